# Optimizing a Trainium2 kernel written in Bass

```python
import math
import jax, jax.numpy as jnp
from jax import lax
import numpy as np

D_MODEL = 2048
BATCH = 4
SEQ = 4096
DEPTH = 2

MEM_LEN = 256
CHUNK = 128
GMLP_WIDTH = 1024
GMLP_GROUPS = 8
GMLP_GROUP_DIM = GMLP_WIDTH // GMLP_GROUPS
N_Q_HEADS = 16
N_KV_HEADS = 4
HEAD_DIM = 64
ATTN_WIDTH = N_Q_HEADS * HEAD_DIM
KV_WIDTH = N_KV_HEADS * HEAD_DIM
WINDOW = 128
ROPE_THETA = 10000.0
X_HEADS = 4
X_HEAD_DIM = 128
X_WIDTH = X_HEADS * X_HEAD_DIM
D_FF = 4 * D_MODEL
LN_EPS = 1e-5
ALPHA = (2 * DEPTH) ** 0.25
BETA = (8 * DEPTH) ** -0.25

OFF_U = GMLP_WIDTH
OFF_V = 2 * GMLP_WIDTH
OFF_Q = OFF_V + ATTN_WIDTH
OFF_K = OFF_Q + KV_WIDTH
OFF_VA = OFF_K + KV_WIDTH
OFF_GA = OFF_VA + D_MODEL
IN_WIDTH = OFF_GA + D_MODEL

kernel_name = "hybrid_gmlp_swa_sink_deepnorm_decoder"


def layer_norm(x, g, b):
    xf = x.astype(jnp.float32)
    mu = jnp.mean(xf, axis=-1, keepdims=True)
    var = jnp.mean(jnp.square(xf - mu), axis=-1, keepdims=True)
    y = (xf - mu) * lax.rsqrt(var + LN_EPS)
    return (y * g.astype(jnp.float32) + b.astype(jnp.float32)).astype(x.dtype)


def rope_tables(seq):
    inv = 1.0 / (ROPE_THETA ** (jnp.arange(0, HEAD_DIM, 2, dtype=jnp.float32) / HEAD_DIM))
    pos = jnp.arange(seq, dtype=jnp.float32)
    ang = pos[:, None] * inv[None, :]
    return jnp.cos(ang), jnp.sin(ang)


def apply_rope(x, cos, sin):
    xf = x.astype(jnp.float32)
    x1, x2 = jnp.split(xf, 2, axis=-1)
    c = cos[None, :, None, :]
    s = sin[None, :, None, :]
    return jnp.concatenate([x1 * c - x2 * s, x2 * c + x1 * s], axis=-1).astype(x.dtype)


def chunked_spatial_gating(u, v, ln_g, ln_b, w_s, b_s):
    bsz, seq, _ = u.shape
    n_chunks = seq // CHUNK
    v = layer_norm(v, ln_g, ln_b)
    vc = v.reshape(bsz, n_chunks, CHUNK, GMLP_GROUPS, GMLP_GROUP_DIM)
    causal = jnp.tril(jnp.ones((CHUNK, CHUNK), dtype=bool))
    w = jnp.where(causal[None], w_s, 0.0)
    mixed = jnp.einsum('gts,bnsgc->bntgc', w, vc) + b_s.T[None, None, :, :, None]
    return u * mixed.reshape(bsz, seq, GMLP_WIDTH)


def sliding_window_attention(q, k, v, sinks):
    bsz, seq, _, _ = q.shape
    n_blk = seq // WINDOW
    grp = N_Q_HEADS // N_KV_HEADS
    qb = q.reshape(bsz, n_blk, WINDOW, N_KV_HEADS, grp, HEAD_DIM)
    kb = k.reshape(bsz, n_blk, WINDOW, N_KV_HEADS, HEAD_DIM)
    vb = v.reshape(bsz, n_blk, WINDOW, N_KV_HEADS, HEAD_DIM)
    pad = ((0, 0), (1, 0), (0, 0), (0, 0), (0, 0))
    kk = jnp.concatenate([jnp.pad(kb, pad)[:, :-1], kb], axis=2)
    vv = jnp.concatenate([jnp.pad(vb, pad)[:, :-1], vb], axis=2)
    scores = jnp.einsum('bnqhgd,bnkhd->bnhgqk', qb, kk).astype(jnp.float32) * (HEAD_DIM ** -0.5)
    q_loc = jnp.arange(WINDOW)[:, None]
    k_loc = jnp.arange(2 * WINDOW)[None, :]
    band = (k_loc <= q_loc + WINDOW) & (k_loc > q_loc)
    blk = jnp.arange(n_blk)[:, None, None]
    valid = band[None] & (blk * WINDOW + k_loc[None] - WINDOW >= 0)
    scores = jnp.where(valid[None, :, None, None], scores, -jnp.inf)
    sink = sinks.astype(jnp.float32).reshape(N_KV_HEADS, grp)[None, None, :, :, None, None]
    m = jnp.maximum(jnp.max(scores, axis=-1, keepdims=True), sink)
    p = jnp.exp(scores - m)
    probs = (p / (jnp.sum(p, axis=-1, keepdims=True) + jnp.exp(sink - m))).astype(v.dtype)
    out = jnp.einsum('bnhgqk,bnkhd->bnqhgd', probs, vv)
    return out.reshape(bsz, seq, ATTN_WIDTH)


def hybrid_mixer(x, w_in, b_gate, ln_v_g, ln_v_b, w_s, b_s, sinks, w_br_a, w_br_b, w_o, cos, sin):
    bsz, seq, _ = x.shape
    proj = x @ w_in
    u, v, q, k, va, ga, gb = jnp.split(proj, [OFF_U, OFF_V, OFF_Q, OFF_K, OFF_VA, OFF_GA], axis=-1)
    ya = chunked_spatial_gating(jax.nn.gelu(u), jax.nn.gelu(v), ln_v_g, ln_v_b, w_s, b_s) @ w_br_a
    q = apply_rope(q.reshape(bsz, seq, N_Q_HEADS, HEAD_DIM), cos, sin)
    k = apply_rope(k.reshape(bsz, seq, N_KV_HEADS, HEAD_DIM), cos, sin)
    va = va.reshape(bsz, seq, N_KV_HEADS, HEAD_DIM)
    yb = sliding_window_attention(q, k, va, sinks) @ w_br_b
    merged = jax.nn.sigmoid(ga + b_gate[:D_MODEL]) * ya + jax.nn.sigmoid(gb + b_gate[D_MODEL:]) * yb
    return merged @ w_o


def memory_cross_attention(x, mem, w_xq, w_xkv, w_xo):
    bsz, seq, _ = x.shape
    q = (x @ w_xq).reshape(bsz, seq, X_HEADS, X_HEAD_DIM)
    k, v = jnp.split(mem @ w_xkv, 2, axis=-1)
    k = k.reshape(bsz, MEM_LEN, X_HEADS, X_HEAD_DIM)
    v = v.reshape(bsz, MEM_LEN, X_HEADS, X_HEAD_DIM)
    s = jnp.einsum('bqhd,bkhd->bhqk', q, k).astype(jnp.float32) * (X_HEAD_DIM ** -0.5)
    p = jax.nn.softmax(s, axis=-1).astype(v.dtype)
    o = jnp.einsum('bhqk,bkhd->bqhd', p, v).reshape(bsz, seq, X_WIDTH)
    return o @ w_xo


def squared_relu_mlp(x, w_up, w_down):
    return jnp.square(jax.nn.relu(x @ w_up)) @ w_down


def setup_inputs(seed: int = 0) -> dict:
    key = jax.random.key(seed)
    ks = jax.random.split(key, 26)
    f32 = jnp.float32
    L, D = DEPTH, D_MODEL

    def nrm(k, shape, scale):
        return jax.random.normal(k, shape, f32) * scale

    return {
        "x": nrm(ks[0], (BATCH, SEQ, D), 1.0),
        "mem": nrm(ks[1], (BATCH, MEM_LEN, D), 1.0),
        "w_in": nrm(ks[2], (L, D, IN_WIDTH), D ** -0.5),
        "b_gate": nrm(ks[3], (L, 2 * D), 0.1),
        "ln_v_g": 1.0 + nrm(ks[4], (L, GMLP_WIDTH), 0.02),
        "ln_v_b": nrm(ks[5], (L, GMLP_WIDTH), 0.02),
        "w_s": nrm(ks[6], (L, GMLP_GROUPS, CHUNK, CHUNK), 0.05),
        "b_s": 1.0 + nrm(ks[7], (L, GMLP_GROUPS, CHUNK), 0.1),
        "sinks": nrm(ks[8], (L, N_Q_HEADS), 0.5),
        "w_br_a": nrm(ks[9], (L, GMLP_WIDTH, D), GMLP_WIDTH ** -0.5),
        "w_br_b": nrm(ks[10], (L, ATTN_WIDTH, D), ATTN_WIDTH ** -0.5),
        "w_o": nrm(ks[11], (L, D, D), BETA * D ** -0.5),
        "ln1_g": 1.0 + nrm(ks[12], (L, D), 0.02),
        "ln1_b": nrm(ks[13], (L, D), 0.02),
        "w_xq": nrm(ks[14], (L, D, X_WIDTH), D ** -0.5),
        "w_xkv": nrm(ks[15], (L, D, 2 * X_WIDTH), D ** -0.5),
        "w_xo": nrm(ks[16], (L, X_WIDTH, D), BETA * X_WIDTH ** -0.5),
        "ln2_g": 1.0 + nrm(ks[17], (L, D), 0.02),
        "ln2_b": nrm(ks[18], (L, D), 0.02),
        "w_up": nrm(ks[19], (L, D, D_FF), D ** -0.5),
        "w_down": nrm(ks[20], (L, D_FF, D), BETA * D_FF ** -0.5),
        "ln3_g": 1.0 + nrm(ks[21], (L, D), 0.02),
        "ln3_b": nrm(ks[22], (L, D), 0.02),
    }


def reference(x, mem, w_in, b_gate, ln_v_g, ln_v_b, w_s, b_s, sinks, w_br_a, w_br_b, w_o,
              ln1_g, ln1_b, w_xq, w_xkv, w_xo, ln2_g, ln2_b, w_up, w_down, ln3_g, ln3_b):
    cos, sin = rope_tables(x.shape[1])
    for l in range(DEPTH):
        y = hybrid_mixer(x, w_in[l], b_gate[l], ln_v_g[l], ln_v_b[l], w_s[l], b_s[l], sinks[l],
                         w_br_a[l], w_br_b[l], w_o[l], cos, sin)
        x = layer_norm(ALPHA * x + y, ln1_g[l], ln1_b[l])
        y = memory_cross_attention(x, mem, w_xq[l], w_xkv[l], w_xo[l])
        x = layer_norm(ALPHA * x + y, ln2_g[l], ln2_b[l])
        y = squared_relu_mlp(x, w_up[l], w_down[l])
        x = layer_norm(ALPHA * x + y, ln3_g[l], ln3_b[l])
    return x
```

```python
import numpy as np
from contextlib import ExitStack
import concourse.bass as bass
import concourse.mybir as mybir
from concourse.bass_utils import run_bass_kernel_spmd

F32 = mybir.dt.float32
BF16 = mybir.dt.bfloat16
AF = mybir.ActivationFunctionType
ALU = mybir.AluOpType

D = 2048
DEPTH = 2
ALPHA = float((2 * DEPTH) ** 0.25)
EPS = 1e-5
NSLOT = 4
NBANK = 6
ENGS = ['pe', 'act', 'dve', 'pool', 'sp']
OFF_V, OFF_Q, OFF_GA, OFF_GB = 1024, 2048, 3584, 5632

DEBUG = {}


class Sched:
    def __init__(self, nc):
        self.nc = nc
        self.prog = {e: [] for e in ENGS}
        self.cnt = {}
        self.sems = {}

    def op(self, eng, fn, deps=(), signal=True):
        tok = None
        if signal:
            self.cnt[eng] = self.cnt.get(eng, 0) + 1
            tok = (eng, self.cnt[eng])
        self.prog[eng].append((fn, [d for d in deps if d is not None], tok, 1))
        return tok

    def dma(self, eng, semname, out, in_, deps=()):
        self.cnt[semname] = self.cnt.get(semname, 0) + 16
        tok = (semname, self.cnt[semname])
        fn = lambda e, out=out, in_=in_: e.dma_start(out=out, in_=in_)
        self.prog[eng].append((fn, [d for d in deps if d is not None], tok, 16))
        return tok

    def emit(self, es):
        nc = self.nc
        names = set(self.cnt.keys()) | set(ENGS)
        for n in sorted(names):
            self.sems[n] = es.enter_context(nc.semaphore("sem_" + n))
        block = es.enter_context(nc.Block())
        engmap = {'pe': block.tensor, 'act': block.scalar, 'dve': block.vector,
                  'pool': block.gpsimd, 'sp': block.sync}
        for ename in ENGS:
            prog = self.prog[ename]

            def body(e, prog=prog):
                waited = {}
                for fn, deps, tok, inc in prog:
                    for (sn, v) in deps:
                        if waited.get(sn, 0) < v:
                            e.wait_ge(self.sems[sn], v)
                            waited[sn] = v
                    ins = fn(e)
                    if tok is not None:
                        ins.then_inc(self.sems[tok[0]], inc)
            engmap[ename](body)


class Ring:
    def __init__(self, aps):
        self.aps = aps
        self.free = [None] * len(aps)
        self.nxt = 0

    def get(self):
        i = self.nxt % len(self.aps)
        self.nxt += 1
        return i, self.aps[i], self.free[i]

    def done(self, i, tok):
        self.free[i] = tok


def build():
    nc = bass.Bass("TRN2", target_bir_lowering=False)

    def din(name, shape):
        return nc.dram_tensor(name, list(shape), F32, kind="ExternalInput").ap()

    xin = din("xin", [2304, D])
    mem = din("mem", [256, D])
    w_in = din("w_in", [2, D, 7680])
    w_kd = din("w_kd", [2, D, 512])
    w_vd = din("w_vd", [2, D, 512])
    w_br_a = din("w_br_a", [2, 1024, D])
    w_br_b = din("w_br_b", [2, 1024, D])
    w_o = din("w_o", [2, D, D])
    w_xq = din("w_xq", [2, D, 512])
    w_xkv = din("w_xkv", [2, D, 1024])
    w_xo = din("w_xo", [2, 512, D])
    w_up = din("w_up", [2, D, 8192])
    w_down = din("w_down", [2, 8192, D])
    lnv_g = din("lnv_g", [2, 1024])
    lnv_b = din("lnv_b", [2, 1024])
    b_s = din("b_s", [2, 1024])
    lng = [din("ln%d_g" % i, [2, D]) for i in (1, 2, 3)]
    lnb = [din("ln%d_b" % i, [2, D]) for i in (1, 2, 3)]
    sinks = din("sinks", [1, 32])
    bgT = din("bgT", [128, 64])
    wsT = din("wsT", [2, 128, 1024])
    consts = din("consts", [128, 5 * 128])
    ropetab = din("ropetab", [2, 128, 2304])
    out = nc.dram_tensor("out", [2048, D], F32, kind="ExternalOutput").ap()
    dbg_out = None
    if DEBUG:
        dbg_out = nc.dram_tensor("dbg", [128, 4 * D], F32, kind="ExternalOutput").ap()

    es = ExitStack()
    with es:
        def sb(name, shape, dt):
            return es.enter_context(nc.sbuf_tensor(name, list(shape), dt))

        xres = sb("xres", [128, 4, D], F32)
        xT = sb("xT", [128, 16, 512], BF16)
        R = sb("R", [128, 24576], BF16)
        W = sb("W", [128, NSLOT, 4096], F32)
        xb = sb("xb", [128, D], BF16)
        cb = sb("cb", [128, 5, 128], BF16)
        cf = sb("cf", [128, 5 * 128], F32)
        ones = sb("ones", [128, 128], BF16)
        rope = sb("rope", [128, 2, 512], F32)
        bg = sb("bg", [128, 64], F32)
        esk = sb("esk", [128, 32], F32)
        WsT = sb("WsT", [128, 2, 1024], BF16)
        kxT = sb("kxT", [128, 2, 4, 256], BF16)
        vx = sb("vx", [128, 2, 2, 512], BF16)
        kprev = sb("kprev", [128, 2, 4, 128], BF16)
        vprev = sb("vprev", [128, 2, 512], BF16)
        wk = sb("wk", [128, 3, 512], F32)
        vg = sb("vg", [128, 1, 1024], F32)
        vn = sb("vn", [128, 2, 1024], BF16)
        pt = sb("pt", [128, 4, 512], BF16)
        stats = sb("stats", [128, 4, 6], F32)
        stats4 = sb("stats4", [128, 4, 4, 6], F32)
        mv = sb("mv", [128, 8], F32)
        epsc = sb("epsc", [128, 1], F32)
        mb = sb("mb", [128, 3, 128], BF16)
        psb = [es.enter_context(nc.psum_tensor("ps%d" % i, [128, 512], F32)) for i in range(NBANK)]
        ptr = [es.enter_context(nc.psum_tensor("ptr%d" % i, [128, 1024], BF16)) for i in range(2)]

        S = Sched(nc)
        banks = Ring([p[:] for p in psb])
        tbanks = Ring([p[:] for p in ptr])
        wkr = Ring([wk[:, i, :] for i in range(3)])
        ptr_ring = Ring([pt[:, i, :] for i in range(4)])
        vgr = Ring([vg[:, i, :] for i in range(1)])
        vnr = Ring([vn[:, i, :] for i in range(2)])
        xbr = Ring([xb[:], vg[:, 0, :].bitcast(BF16)])

        ident = cb[:, 0, :]
        rperm = cb[:, 1, :]
        maskC = cb[:, 2, :]
        maskP = cb[:, 3, :]
        maskP0 = cb[:, 4, :]

        def rv(off, c, n=512):
            return R[:, off:off + c * n].rearrange("p (c t) -> p c t", c=c)
        uT = rv(0, 8)
        qT = rv(4096, 8)
        mg = rv(0, 16)
        zT = rv(8192, 8)
        oT = rv(12288, 8)
        kTa = R[:, 16384:16384 + 4 * 640].rearrange("p (c t) -> p c t", c=4)
        Vd = R[:, 18944:18944 + 5 * 512].rearrange("p (c t) -> p c t", c=5)
        kTb = R[:, 21504:21504 + 4 * 640].rearrange("p (c t) -> p c t", c=4)
        hT = rv(0, 32)
        xqT = rv(0, 4)
        oxT = rv(2048, 4)
        memT = R[:, 0:16 * 256].rearrange("p (c t) -> p c t", c=16)

        slot_free = [None] * NSLOT
        slot_n = [0]

        def slot_bf(s):
            return W[:, s, :].bitcast(BF16)

        def load_slot(parts, queue='pool'):
            s = slot_n[0] % NSLOT
            slot_n[0] += 1
            tok = None
            for dst_fn, src in parts:
                tok = S.dma(queue, ('w%d' if queue == 'pool' else 'v%d') % s, dst_fn(s), src, [slot_free[s]])
            return s, tok

        def wblock(src, kc, n, off=0):
            return (lambda s: slot_bf(s)[:, off:off + kc * n].rearrange("p (k n) -> p k n", k=kc),
                    src.rearrange("(k p) n -> p k n", p=128))

        def wview(s, kc, n, off=0):
            return slot_bf(s)[:, off:off + kc * n].rearrange("p (k n) -> p k n", k=kc)

        state = {'xres': [None] * 4, 'xT_w': [None] * 4, 'pe_last': None}

        def mm_group(out_ap, pairs, deps, bank_dep):
            n = len(pairs)
            tok = None
            for i, (l, r) in enumerate(pairs):
                tok = S.op('pe', lambda e, l=l, r=r, i=i: e.matmul(out_ap, lhsT=l, rhs=r, start=(i == 0),
                                                                  stop=(i == n - 1)),
                           (list(deps) + [bank_dep]) if i == 0 else [], signal=(i == n - 1))
            state['pe_last'] = tok
            return tok

        def ln_epilogue(l, which, nt, last_layer_out=None, need_T=True):
            s, ltok = load_slot([
                (lambda s: W[:, s, 0:D], lng[which][l].partition_broadcast(128)),
                (lambda s: W[:, s, D:2 * D], lnb[which][l].partition_broadcast(128))], queue='sp')
            gbc = W[:, s, 0:D]
            bbc = W[:, s, D:2 * D]
            lst = {'last': None}

            def stage_a(t):
                xt_ = xres[:, t, :]
                d = state['xres'][t]
                d = S.op('dve', lambda e, t=t: e.bn_aggr(out=mv[:, 0:2], in_=stats4[:, t, :, :].rearrange("p a b -> p (a b)")),
                         [d, lst['last']])
                d = S.op('act', lambda e: e.activation(out=mv[:, 2:3], in_=mv[:, 1:2], func=AF.Sqrt, bias=epsc[:, 0:1]), [d])
                d = S.op('dve', lambda e: e.reciprocal(out=mv[:, 2:3], in_=mv[:, 2:3]), [d])
                d = S.op('dve', lambda e, xt_=xt_: e.scalar_tensor_tensor(out=xt_, in0=xt_, scalar=mv[:, 0:1], in1=gbc,
                                                                           op0=ALU.subtract, op1=ALU.mult), [d, ltok])
                d = S.op('dve', lambda e, xt_=xt_: e.scalar_tensor_tensor(out=xt_, in0=xt_, scalar=mv[:, 2:3], in1=bbc,
                                                                           op0=ALU.mult, op1=ALU.add), [d])
                lst['last'] = d
                if last_layer_out is not None and last_layer_out[t] is not None:
                    d = S.dma('sp', 'do%d' % t, last_layer_out[t], xt_, [d])
                info = None
                if need_T:
                    xi, xbuf, xdep = xbr.get()
                    d2 = S.op('act', lambda e, xt_=xt_, xbuf=xbuf: e.activation(out=xbuf, in_=xt_, func=AF.Copy),
                              [lst['last'], xdep])
                    info = (xi, xbuf, d2)
                    d = d2
                state['xres'][t] = d
                return info

            def stage_b(t, info):
                xi, xbuf, d2 = info
                tl = None
                cs = []
                for hb in range(2):
                    bi, bap, bdep = tbanks.get()
                    for j in range(8):
                        kc = hb * 8 + j
                        tl = S.op('pe', lambda e, bap=bap, j=j, kc=kc, xbuf=xbuf: e.transpose(
                            bap[:, j * 128:(j + 1) * 128], xbuf[:, kc * 128:(kc + 1) * 128], ident),
                            [d2, bdep] if j == 0 else [], signal=(j == 7))
                    if hb == 0:
                        c = S.op('dve', lambda e, bap=bap, hb=hb, t=t: e.tensor_copy(
                            out=xT[:, hb * 8:(hb + 1) * 8, t * 128:(t + 1) * 128],
                            in_=bap.rearrange("p (c t) -> p c t", c=8)), [tl, state['pe_last']])
                    else:
                        c = S.op('act', lambda e, bap=bap, hb=hb, t=t: e.activation(
                            out=xT[:, hb * 8:(hb + 1) * 8, t * 128:(t + 1) * 128],
                            in_=bap.rearrange("p (c t) -> p c t", c=8), func=AF.Copy), [tl, state['pe_last']])
                    tbanks.done(bi, c)
                    cs.append(c)
                xbr.done(xi, tl)
                state['xT_w'][t] = tuple(cs)

            ctx = {'prev': None}

            def ln_tile(t):
                info = stage_a(t)
                if ctx['prev'] is not None and ctx['prev'][1] is not None:
                    stage_b(*ctx['prev'])
                ctx['prev'] = (t, info)

            def ln_end():
                if ctx['prev'][1] is not None:
                    stage_b(*ctx['prev'])
                slot_free[s] = lst['last']

            return ln_tile, ln_end

        def acc_stats(t, ob, a):
            return S.op('dve', lambda e, t=t, ob=ob: e.bn_stats(out=stats4[:, t, ob, :], in_=xres[:, t, ob * 512:(ob + 1) * 512]), [a])

        def xT_deps(nt):
            r = []
            for t in range(nt):
                if state['xT_w'][t] is not None:
                    r += list(state['xT_w'][t])
            return r

        t_c = S.dma('sp', 'dc', cf[:], consts)
        t_cb = S.op('act', lambda e: e.activation(out=cb[:].rearrange("p a b -> p (a b)"), in_=cf[:], func=AF.Copy), [t_c])
        t_ones = S.op('dve', lambda e: e.memset(ones[:], 1.0))
        t_ones = S.op('dve', lambda e: e.memset(epsc[:], EPS), [t_ones])
        t_mb = None
        for i in range(3):
            t_mb = S.op('dve', lambda e, i=i: e.tensor_scalar(out=mb[:, i, :], in0=cf[:, (2 + i) * 128:(3 + i) * 128], scalar1=-1.0,
                                                             scalar2=30000.0, op0=ALU.add, op1=ALU.mult), [t_c, t_mb])
        t_bg = S.dma('sp', 'dbg', bg[:], bgT)
        t_sk = S.dma('sp', 'dsk', esk[:], sinks.partition_broadcast(128))
        t_esk = S.op('act', lambda e: e.activation(out=esk[:], in_=esk[:], func=AF.Exp), [t_sk])
        t_kp = S.op('dve', lambda e: e.memset(kprev[:].rearrange("p a b c -> p (a b c)"), 0.0))
        t_vp = S.op('dve', lambda e: e.memset(vprev[:].rearrange("p a b -> p (a b)"), 0.0))
        state['kprev'] = [t_kp, t_kp]
        state['vprev'] = [t_vp, t_vp]
        t_ws_all = []
        for l in range(2):
            i, wap, wdep = wkr.get()
            i2, wap2, wdep2 = wkr.get()
            t1 = S.dma('sp', 'dws%d' % l, wap, wsT[l][:, 0:512], [wdep])
            t2 = S.dma('sp', 'dws%d' % l, wap2, wsT[l][:, 512:1024], [wdep2])
            a = S.op('dve', lambda e, wap=wap, l=l: e.tensor_tensor(
                out=WsT[:, l, 0:512].rearrange("p (g t) -> p g t", g=4), in0=wap.rearrange("p (g t) -> p g t", g=4),
                in1=cf[:, 256:384].unsqueeze(1).to_broadcast([128, 4, 128]), op=ALU.mult), [t2, t_c])
            b = S.op('dve', lambda e, wap2=wap2, l=l: e.tensor_tensor(
                out=WsT[:, l, 512:1024].rearrange("p (g t) -> p g t", g=4), in0=wap2.rearrange("p (g t) -> p g t", g=4),
                in1=cf[:, 256:384].unsqueeze(1).to_broadcast([128, 4, 128]), op=ALU.mult), [a])
            wkr.done(i, a)
            wkr.done(i2, b)
            t_ws_all.append(b)
        mt = []
        for mc in range(2):
            d = S.dma('sp', 'dx%d' % mc, xres[:, mc, :], mem[mc * 128:(mc + 1) * 128, :])
            xi, xbuf, xdep = xbr.get()
            d2 = S.op('act', lambda e, mc=mc, xbuf=xbuf: e.activation(out=xbuf, in_=xres[:, mc, :], func=AF.Copy),
                      [d, xdep])
            tl = None
            for hb in range(2):
                bi, bap, bdep = tbanks.get()
                for j in range(8):
                    kc = hb * 8 + j
                    tl = S.op('pe', lambda e, bap=bap, j=j, kc=kc, xbuf=xbuf: e.transpose(
                        bap[:, j * 128:(j + 1) * 128], xbuf[:, kc * 128:(kc + 1) * 128], ident),
                        [d2, bdep, t_cb] if j == 0 else [], signal=(j == 7))
                c = S.op('dve', lambda e, bap=bap, hb=hb, mc=mc: e.tensor_copy(
                    out=memT[:, hb * 8:(hb + 1) * 8, mc * 128:(mc + 1) * 128],
                    in_=bap.rearrange("p (c t) -> p c t", c=8)), [tl])
                tbanks.done(bi, c)
                mt.append(c)
            xbr.done(xi, tl)
            state['xres'][mc] = tl
        for l in range(2):
            s, lt = load_slot([wblock(w_xkv[l][:, 0:512], 16, 512)])
            wv = wview(s, 16, 512)
            tk = None
            for h in range(4):
                bi, bap, bdep = banks.get()
                tk = mm_group(bap[:, 0:256], [(wv[:, kc, h * 128:(h + 1) * 128], memT[:, kc, :]) for kc in range(16)],
                              [lt] + mt, bdep)
                c = S.op('act', lambda e, bap=bap, l=l, h=h: e.activation(out=kxT[:, l, h, :], in_=bap[:, 0:256],
                                                                         func=AF.Copy), [tk])
                banks.done(bi, c)
            slot_free[s] = tk
            s, lt = load_slot([wblock(w_xkv[l][:, 512:1024], 16, 512)])
            wv = wview(s, 16, 512)
            for mc in range(2):
                bi, bap, bdep = banks.get()
                tk = mm_group(bap, [(memT[:, kc, mc * 128:(mc + 1) * 128], wv[:, kc, :]) for kc in range(16)],
                              [lt] + mt, bdep)
                c = S.op('act', lambda e, bap=bap, l=l, mc=mc: e.activation(out=vx[:, l, mc, :], in_=bap,
                                                                          func=AF.Copy), [tk])
                banks.done(bi, c)
            slot_free[s] = tk
        state['pro'] = [t_cb, t_ones, t_bg, t_esk, t_mb] + t_ws_all

        def rope_evac(bap, tk, dst, n, rtok):
            i0, qs, qd = ptr_ring.get()
            a = S.op('act', lambda e: e.activation(out=qs[:, 0:n], in_=bap[:, 0:n], func=AF.Copy), [tk, qd])
            i1, t1, t1d = wkr.get()
            b = S.op('dve', lambda e: e.tensor_tensor(out=t1[:, 0:n], in0=bap[:, 0:n], in1=rope[:, 0, 0:n], op=ALU.mult),
                     [tk, t1d, rtok, a])
            b2i, bap2, bdep2 = banks.get()
            r = mm_group(bap2[:, 0:n], [(rperm, qs[:, 0:n])], [a], bdep2)
            ptr_ring.done(i0, r)
            i2, t2, t2d = wkr.get()
            c = S.op('dve', lambda e: e.tensor_tensor(out=t2[:, 0:n], in0=bap2[:, 0:n], in1=rope[:, 1, 0:n], op=ALU.mult),
                     [r, t2d])
            banks.done(b2i, c)
            if isinstance(dst, list):
                d = None
                for (dap, p0, p1) in dst:
                    d = S.op('dve', lambda e, dap=dap, p0=p0, p1=p1: e.tensor_tensor(
                        out=dap, in0=t1[p0:p1, 0:n], in1=t2[p0:p1, 0:n], op=ALU.add), [b, c, d])
            else:
                d = S.op('dve', lambda e: e.tensor_tensor(out=dst, in0=t1[:, 0:n], in1=t2[:, 0:n], op=ALU.add), [b, c])
            wkr.done(i1, d)
            wkr.done(i2, d)
            return b, d

        def group_layer(l, nt, row0, kv_only, first_mask_special, out_rows):
            n = nt * 128
            pro = state['pro']
            rtok = S.dma('sp', 'drope', rope[:, 0, 0:n], ropetab[0][:, row0:row0 + n], [state.get('rope_r')])
            rtok = S.dma('sp', 'drope', rope[:, 1, 0:n], ropetab[1][:, row0:row0 + n], [state.get('rope_r')])
            xd = xT_deps(nt)
            if DEBUG.get('stop') == 'rope':
                return 'stop'
            if not kv_only:
                tu = None
                for ub in range(2):
                    s, lt = load_slot([wblock(w_in[l][:, ub * 512:(ub + 1) * 512], 16, 512)])
                    wv = wview(s, 16, 512)
                    tk = None
                    for hc in range(4):
                        bi, bap, bdep = banks.get()
                        tk = mm_group(bap[:, 0:n], [(wv[:, kc, hc * 128:(hc + 1) * 128], xT[:, kc, 0:n]) for kc in range(16)],
                                      [lt] + xd, bdep)
                        tu = S.op('act', lambda e, bap=bap, c=ub * 4 + hc: e.activation(
                            out=uT[:, c, 0:n], in_=bap[:, 0:n], func=AF.Gelu_apprx_tanh), [tk])
                        banks.done(bi, tu)
                    slot_free[s] = tk
                if DEBUG.get('stop') == 'A1':
                    return 'stop'
                s0, lt0 = load_slot([wblock(w_in[l][:, OFF_V:OFF_V + 512], 16, 512)])
                s1, lt1 = load_slot([wblock(w_in[l][:, OFF_V + 512:OFF_V + 1024], 16, 512)])
                sv, ltv = load_slot([
                    (lambda s: W[:, s, 0:1024], lnv_g[l].partition_broadcast(128)),
                    (lambda s: W[:, s, 1024:2048], lnv_b[l].partition_broadcast(128)),
                    (lambda s: W[:, s, 2048:3072], b_s[l].partition_broadcast(128))], queue='sp')
                wvs = [wview(s0, 16, 512), wview(s1, 16, 512)]
                a2 = {'tkv': None, 'tz': None}

                def v_stage(t):
                    gi, gap, gdep = vgr.get()
                    d = None
                    for vb in range(2):
                        bi, bap, bdep = banks.get()
                        a2['tkv'] = mm_group(bap, [(xT[:, kc, t * 128:(t + 1) * 128], wvs[vb][:, kc, :]) for kc in range(16)],
                                             [lt0, lt1] + xd, bdep)
                        a = S.op('act', lambda e, bap=bap, gap=gap, vb=vb: e.activation(
                            out=gap[:, vb * 512:(vb + 1) * 512], in_=bap, func=AF.Gelu_apprx_tanh), [a2['tkv'], gdep])
                        banks.done(bi, a)
                        d = S.op('dve', lambda e, gap=gap, vb=vb: e.bn_stats(out=stats[:, vb, :], in_=gap[:, vb * 512:(vb + 1) * 512]),
                                 [a, d, state.get('stats_r')])
                    d = S.op('dve', lambda e: e.bn_aggr(out=mv[:, 4:6], in_=stats[:, 0:2, :].rearrange("p a b -> p (a b)")), [d])
                    state['stats_r'] = d
                    d = S.op('act', lambda e: e.activation(out=mv[:, 6:7], in_=mv[:, 5:6], func=AF.Sqrt, bias=epsc[:, 0:1]), [d])
                    d = S.op('dve', lambda e: e.reciprocal(out=mv[:, 6:7], in_=mv[:, 6:7]), [d])
                    d = S.op('dve', lambda e, gap=gap: e.scalar_tensor_tensor(out=gap, in0=gap, scalar=mv[:, 4:5], in1=W[:, sv, 0:1024],
                                                                             op0=ALU.subtract, op1=ALU.mult), [d, ltv])
                    ni, nap, ndep = vnr.get()
                    d = S.op('dve', lambda e, gap=gap, nap=nap: e.scalar_tensor_tensor(out=nap, in0=gap, scalar=mv[:, 6:7],
                                                                                      in1=W[:, sv, 1024:2048], op0=ALU.mult,
                                                                                      op1=ALU.add), [d, ndep])
                    vgr.done(gi, d)
                    return (t, ni, nap, d)

                def a3_stage(t, ni, nap, d):
                    tm = None
                    for half in range(2):
                        bi, bap, bdep = banks.get()
                        for gq in range(4):
                            g = half * 4 + gq
                            tm = mm_group(bap[:, gq * 128:(gq + 1) * 128],
                                          [(nap[:, g * 128:(g + 1) * 128], WsT[:, l, g * 128:(g + 1) * 128])],
                                          [d] + pro, bdep if gq == 0 else None)
                        wi, wap, wdep = wkr.get()
                        a = S.op('dve', lambda e, bap=bap, wap=wap, half=half: e.tensor_tensor(
                            out=wap, in0=bap, in1=W[:, sv, 2048 + half * 512:2048 + (half + 1) * 512], op=ALU.add),
                            [tm, wdep, ltv])
                        banks.done(bi, a)
                        a2['tz'] = S.op('dve', lambda e, wap=wap, half=half, t=t: e.tensor_tensor(
                            out=zT[:, half * 4:(half + 1) * 4, t * 128:(t + 1) * 128],
                            in0=wap.rearrange("p (g t) -> p g t", g=4),
                            in1=uT[:, half * 4:(half + 1) * 4, t * 128:(t + 1) * 128], op=ALU.mult), [a, tu])
                        wkr.done(wi, a2['tz'])
                    vnr.done(ni, tm)

                pv = None
                for t in range(nt):
                    cur = v_stage(t)
                    if pv is not None:
                        a3_stage(*pv)
                    pv = cur
                a3_stage(*pv)
                tkv = a2['tkv']
                tz = a2['tz']
                slot_free[s0] = tkv
                slot_free[s1] = tkv
                slot_free[sv] = tz
                if DEBUG.get('stop') == 'A3':
                    return 'stop'
                tq = None
                for qb in range(2):
                    s, lt = load_slot([wblock(w_in[l][:, OFF_Q + qb * 512:OFF_Q + (qb + 1) * 512], 16, 512)])
                    wv = wview(s, 16, 512)
                    tk = None
                    for hc in range(4):
                        bi, bap, bdep = banks.get()
                        tk = mm_group(bap[:, 0:n], [(wv[:, kc, hc * 128:(hc + 1) * 128], xT[:, kc, 0:n]) for kc in range(16)],
                                      [lt] + xd, bdep)
                        b, tq = rope_evac(bap, tk, qT[:, qb * 4 + hc, 0:n], n, rtok)
                        banks.done(bi, tq)
                    slot_free[s] = tk
            if DEBUG.get('stop') == 'A4':
                return 'stop'
            s, lt = load_slot([wblock(w_kd[l], 16, 512)])
            wv = wview(s, 16, 512)
            tkk = []
            zk = S.op('dve', lambda e: e.memset(kTa[64:128, :, :], 0.0), [state['pe_last']])
            zk = S.op('dve', lambda e: e.memset(kTb[0:64, :, :], 0.0), [state['pe_last'], zk])
            cpk = S.op('act', lambda e: e.activation(out=kTa[0:64, :, 0:128], in_=kprev[0:64, l, :, :], func=AF.Copy),
                       [state['kprev'][l], state['pe_last']])
            cpk = S.op('act', lambda e: e.activation(out=kTb[64:128, :, 0:128], in_=kprev[64:128, l, :, :], func=AF.Copy),
                       [state['kprev'][l], state['pe_last'], cpk])
            cpv = S.op('act', lambda e: e.activation(out=Vd[:, 0, :], in_=vprev[:, l, :], func=AF.Copy),
                       [state['vprev'][l], state['pe_last']])
            tk = None
            for m in range(4):
                bi, bap, bdep = banks.get()
                tk = mm_group(bap[:, 0:n], [(wv[:, kc, m * 128:(m + 1) * 128], xT[:, kc, 0:n]) for kc in range(16)],
                              [lt] + xd, bdep)
                b, tkd = rope_evac(bap, tk, [(kTa[0:64, m, 128:128 + n], 0, 64), (kTb[64:128, m, 128:128 + n], 64, 128)], n, rtok)
                banks.done(bi, tkd)
                tkk.append(tkd)
            slot_free[s] = tk
            state['rope_r'] = tkk[-1]
            s, lt = load_slot([wblock(w_vd[l], 16, 512)])
            wv = wview(s, 16, 512)
            tvv = []
            for t in range(nt):
                bi, bap, bdep = banks.get()
                tk = mm_group(bap, [(xT[:, kc, t * 128:(t + 1) * 128], wv[:, kc, :]) for kc in range(16)], [lt] + xd, bdep)
                c = S.op('act', lambda e, bap=bap, t=t: e.activation(out=Vd[:, 1 + t, :], in_=bap, func=AF.Copy), [tk])
                banks.done(bi, c)
                tvv.append(c)
            slot_free[s] = tk
            sk = S.op('act', lambda e: e.activation(out=kprev[0:64, l, :, :], in_=kTa[0:64, :, n:n + 128], func=AF.Copy),
                      tkk + [cpk])
            sk = S.op('act', lambda e: e.activation(out=kprev[64:128, l, :, :], in_=kTb[64:128, :, n:n + 128], func=AF.Copy),
                      tkk + [cpk, sk])
            sv_ = S.op('act', lambda e: e.activation(out=vprev[:, l, :], in_=Vd[:, nt, :], func=AF.Copy), [tvv[-1], cpv])
            state['kprev'][l] = sk
            state['vprev'][l] = sv_
            if kv_only:
                return
            if DEBUG.get('stop') == 'A5':
                return 'stop'
            att = {'last': None}

            def att_stage1(t, m):
                pts = []
                for kb in range(2):
                    bi, bap, bdep = banks.get()
                    if kb == 1:
                        mi = 0
                    else:
                        mi = 2 if (first_mask_special and t == 0) else 1
                    S.op('pe', lambda e, bap=bap, mi=mi: e.matmul(
                        bap.rearrange("p (j q) -> p j q", j=4), lhsT=ident,
                        rhs=mb[:, mi, :].unsqueeze(1).to_broadcast([128, 4, 128]), start=True, stop=False,
                        skip_group_check=True), tkk + [tq, cpk, zk, bdep] + pro, signal=False)
                    tk = None
                    for half in range(2):
                        kTh = kTa if half == 0 else kTb
                        for j in range(2):
                            blk = half * 2 + j
                            tk = S.op('pe', lambda e, bap=bap, blk=blk, kTh=kTh, j=j, kb=kb: e.matmul(
                                bap[:, blk * 128:(blk + 1) * 128], lhsT=kTh[:, m, (t + kb) * 128:(t + kb + 1) * 128],
                                rhs=qT[:, 2 * m + j, t * 128:(t + 1) * 128], start=False, stop=True,
                                skip_group_check=True), [], signal=(blk == 3))
                    state['pe_last'] = tk
                    pi, pap, pdep = ptr_ring.get()
                    a = S.op('act', lambda e, bap=bap, pap=pap: e.activation(out=pap, in_=bap, func=AF.Exp, scale=0.125),
                             [tk, pdep])
                    banks.done(bi, a)
                    pts.append((pi, pap, a))
                return pts

            def att_stage2(t, m, pts):
                bo, bapo, bdepo = banks.get()
                tko = mm_group(bapo, [(Vd[:, t + kb, m * 128:(m + 1) * 128], pts[kb][1]) for kb in range(2)],
                               [pts[0][2], pts[1][2], cpv] + tvv, bdepo)
                bs_, baps, bdeps = banks.get()
                tks = mm_group(baps, [(ones[:], pts[kb][1]) for kb in range(2)], pro, bdeps)
                for kb in range(2):
                    ptr_ring.done(pts[kb][0], tks)
                wi, wap, wdep = wkr.get()
                dd = S.op('dve', lambda e, baps=baps, wap=wap: e.tensor_tensor(
                    out=wap.rearrange("p (b q) -> p b q", b=4), in0=baps.rearrange("p (b q) -> p b q", b=4),
                    in1=esk[:, l * 16 + 4 * m:l * 16 + 4 * m + 4].unsqueeze(2).to_broadcast([128, 4, 128]), op=ALU.add),
                    [tks, wdep] + pro)
                banks.done(bs_, dd)
                dd = S.op('act', lambda e, wap=wap: e.activation(out=wap, in_=wap, func=AF.Ln), [dd])
                dd = S.op('act', lambda e, wap=wap: e.activation(out=wap, in_=wap, func=AF.Exp, scale=-1.0), [dd])
                tl_ = None
                for half in range(2):
                    p0 = half * 64
                    tl_ = S.op('dve', lambda e, bapo=bapo, wap=wap, half=half, p0=p0, m=m, t=t: e.tensor_tensor(
                        out=oT[p0:p0 + 64, 2 * m:2 * m + 2, t * 128:(t + 1) * 128],
                        in0=bapo[p0:p0 + 64, half * 256:(half + 1) * 256].rearrange("p (j q) -> p j q", j=2),
                        in1=wap[p0:p0 + 64, half * 256:(half + 1) * 256].rearrange("p (j q) -> p j q", j=2),
                        op=ALU.mult), [tko, dd])
                banks.done(bo, tl_)
                wkr.done(wi, tl_)
                att['last'] = tl_

            prev_item = None
            for t in range(nt):
                for m in range(4):
                    pts = att_stage1(t, m)
                    if prev_item is not None:
                        att_stage2(*prev_item)
                    prev_item = (t, m, pts)
            att_stage2(*prev_item)
            to_last = att['last']
            if DEBUG.get('stop') == 'A6':
                return 'stop'
            tmg = None
            for i in range(8):
                c0 = i * 256
                sg, ltg = load_slot([wblock(w_in[l][:, OFF_GA + c0:OFF_GA + c0 + 256], 16, 256, 0),
                                     wblock(w_in[l][:, OFF_GB + c0:OFF_GB + c0 + 256], 16, 256, 4096)])
                sa, lta = load_slot([wblock(w_br_a[l][:, c0:c0 + 256], 8, 256, 0),
                                     wblock(w_br_b[l][:, c0:c0 + 256], 8, 256, 2048)])
                wga = wview(sg, 16, 256, 0)
                wgb = wview(sg, 16, 256, 4096)
                wa = wview(sa, 8, 256, 0)
                wb_ = wview(sa, 8, 256, 2048)
                tk = None
                for cc in range(2):
                    c = i * 2 + cc
                    cs = slice(cc * 128, (cc + 1) * 128)
                    b1, ba, d1 = banks.get()
                    tka = mm_group(ba[:, 0:n], [(wa[:, kc, cs], zT[:, kc, 0:n]) for kc in range(8)], [lta, tz], d1)
                    b2, bb, d2 = banks.get()
                    tkb = mm_group(bb[:, 0:n], [(wb_[:, kc, cs], oT[:, kc, 0:n]) for kc in range(8)], [lta, to_last], d2)
                    b3, bga, d3 = banks.get()
                    tkga = mm_group(bga[:, 0:n], [(wga[:, kc, cs], xT[:, kc, 0:n]) for kc in range(16)], [ltg] + xd, d3)
                    b4, bgb, d4 = banks.get()
                    tk = mm_group(bgb[:, 0:n], [(wgb[:, kc, cs], xT[:, kc, 0:n]) for kc in range(16)], [ltg] + xd, d4)
                    w1, sga, wd1 = wkr.get()
                    a1 = S.op('act', lambda e, bga=bga, sga=sga, c=c: e.activation(
                        out=sga[:, 0:n], in_=bga[:, 0:n], func=AF.Sigmoid, bias=bg[:, l * 32 + c:l * 32 + c + 1]),
                        [tkga, wd1] + pro)
                    banks.done(b3, a1)
                    w2, sgb, wd2 = wkr.get()
                    a2 = S.op('act', lambda e, bgb=bgb, sgb=sgb, c=c: e.activation(
                        out=sgb[:, 0:n], in_=bgb[:, 0:n], func=AF.Sigmoid, bias=bg[:, l * 32 + 16 + c:l * 32 + 16 + c + 1]),
                        [tk, wd2] + pro)
                    banks.done(b4, a2)
                    m1 = S.op('dve', lambda e, ba=ba, sga=sga: e.tensor_tensor(out=sga[:, 0:n], in0=ba[:, 0:n], in1=sga[:, 0:n],
                                                                            op=ALU.mult), [tka, a1])
                    banks.done(b1, m1)
                    m2 = S.op('dve', lambda e, bb=bb, sgb=sgb: e.tensor_tensor(out=sgb[:, 0:n], in0=bb[:, 0:n], in1=sgb[:, 0:n],
                                                                            op=ALU.mult), [tkb, a2])
                    banks.done(b2, m2)
                    tmg = S.op('dve', lambda e, sga=sga, sgb=sgb, c=c: e.tensor_tensor(out=mg[:, c, 0:n], in0=sga[:, 0:n],
                                                                                    in1=sgb[:, 0:n], op=ALU.add), [m1, m2])
                    wkr.done(w1, tmg)
                    wkr.done(w2, tmg)
                slot_free[sg] = tk
                slot_free[sa] = tk
            if DEBUG.get('stop') == 'A7':
                return 'stop'
            for ob in range(4):
                s, lt = load_slot([wblock(w_o[l][:, ob * 512:(ob + 1) * 512], 16, 512)])
                wv = wview(s, 16, 512)
                if ob == 3:
                    ln_tile, ln_end = ln_epilogue(l, 0, nt)
                tk = None
                for t in range(nt):
                    bi, bap, bdep = banks.get()
                    tk = mm_group(bap, [(mg[:, kc, t * 128:(t + 1) * 128], wv[:, kc, :]) for kc in range(16)], [lt, tmg], bdep)
                    a = S.op('dve', lambda e, bap=bap, t=t, ob=ob: e.scalar_tensor_tensor(
                        out=xres[:, t, ob * 512:(ob + 1) * 512], in0=xres[:, t, ob * 512:(ob + 1) * 512], scalar=ALPHA,
                        in1=bap, op0=ALU.mult, op1=ALU.add), [tk, state['xres'][t]])
                    banks.done(bi, a)
                    state['xres'][t] = acc_stats(t, ob, a)
                    if ob == 3:
                        ln_tile(t)
                slot_free[s] = tk
            ln_end()
            if DEBUG.get('stop') == 'ln1' and DEBUG.get('gl') == (row0, l):
                return 'stop'
            xd = xT_deps(nt)
            s, lt = load_slot([wblock(w_xq[l], 16, 512)])
            wv = wview(s, 16, 512)
            tk = None
            tqs = []
            for h in range(4):
                bi, bap, bdep = banks.get()
                tk = mm_group(bap[:, 0:n], [(wv[:, kc, h * 128:(h + 1) * 128], xT[:, kc, 0:n]) for kc in range(16)],
                              [lt] + xd, bdep)
                c = S.op('act', lambda e, bap=bap, h=h: e.activation(out=xqT[:, h, 0:n], in_=bap[:, 0:n], func=AF.Copy), [tk])
                banks.done(bi, c)
                tqs.append(c)
            slot_free[s] = tk
            tox = None
            for h in range(4):
                pts = []
                for mc in range(2):
                    bi, bap, bdep = banks.get()
                    tk = mm_group(bap[:, 0:n], [(kxT[:, l, h, mc * 128:(mc + 1) * 128], xqT[:, h, 0:n])], [tqs[h]], bdep)
                    pi, pap, pdep = ptr_ring.get()
                    a = S.op('act', lambda e, bap=bap, pap=pap: e.activation(out=pap[:, 0:n], in_=bap[:, 0:n], func=AF.Exp,
                                                                             scale=float(128 ** -0.5)), [tk, pdep])
                    banks.done(bi, a)
                    pts.append((pi, pap, a))
                bo, bapo, bdepo = banks.get()
                tko = mm_group(bapo[:, 0:n], [(vx[:, l, mc, h * 128:(h + 1) * 128], pts[mc][1][:, 0:n]) for mc in range(2)],
                               [pts[0][2], pts[1][2]], bdepo)
                bs_, baps, bdeps = banks.get()
                tks = mm_group(baps[:, 0:n], [(ones[:], pts[mc][1][:, 0:n]) for mc in range(2)], pro, bdeps)
                for mc in range(2):
                    ptr_ring.done(pts[mc][0], tks)
                wi, wap, wdep = wkr.get()
                dd = S.op('act', lambda e, baps=baps, wap=wap: e.activation(out=wap[:, 0:n], in_=baps[:, 0:n], func=AF.Ln), [tks, wdep])
                dd = S.op('act', lambda e, wap=wap: e.activation(out=wap[:, 0:n], in_=wap[:, 0:n], func=AF.Exp, scale=-1.0), [dd])
                banks.done(bs_, dd)
                tox = S.op('dve', lambda e, bapo=bapo, wap=wap, h=h: e.tensor_tensor(
                    out=oxT[:, h, 0:n], in0=bapo[:, 0:n], in1=wap[:, 0:n], op=ALU.mult), [tko, dd])
                banks.done(bo, tox)
                wkr.done(wi, tox)
            s, lt = load_slot([wblock(w_xo[l], 4, 2048)])
            wv = wview(s, 4, 2048)
            ln_tile, ln_end = ln_epilogue(l, 1, nt)
            tk = None
            for t in range(nt):
                for ob in range(4):
                    bi, bap, bdep = banks.get()
                    tk = mm_group(bap, [(oxT[:, kc, t * 128:(t + 1) * 128], wv[:, kc, ob * 512:(ob + 1) * 512]) for kc in range(4)],
                                  [lt, tox], bdep)
                    a = S.op('dve', lambda e, bap=bap, t=t, ob=ob: e.scalar_tensor_tensor(
                        out=xres[:, t, ob * 512:(ob + 1) * 512], in0=xres[:, t, ob * 512:(ob + 1) * 512], scalar=ALPHA,
                        in1=bap, op0=ALU.mult, op1=ALU.add), [tk, state['xres'][t]])
                    banks.done(bi, a)
                    state['xres'][t] = acc_stats(t, ob, a)
                ln_tile(t)
            slot_free[s] = tk
            ln_end()
            if DEBUG.get('stop') == 'ln2' and DEBUG.get('gl') == (row0, l):
                return 'stop'
            xd = xT_deps(nt)
            th = None
            for hh in range(2):
                for hb in range(hh * 8, hh * 8 + 8):
                    s, lt = load_slot([wblock(w_up[l][:, hb * 512:(hb + 1) * 512], 16, 512)])
                    wv = wview(s, 16, 512)
                    tk = None
                    for hc in range(4):
                        bi, bap, bdep = banks.get()
                        tk = mm_group(bap[:, 0:n], [(wv[:, kc, hc * 128:(hc + 1) * 128], xT[:, kc, 0:n]) for kc in range(16)],
                                      [lt] + xd, bdep)
                        wi, wap, wdep = wkr.get()
                        a = S.op('act', lambda e, bap=bap, wap=wap: e.activation(out=wap[:, 0:n], in_=bap[:, 0:n], func=AF.Relu),
                                 [tk, wdep])
                        banks.done(bi, a)
                        th = S.op('dve', lambda e, wap=wap, c=(hb - hh * 8) * 4 + hc: e.tensor_tensor(
                            out=hT[:, c, 0:n], in0=wap[:, 0:n], in1=wap[:, 0:n], op=ALU.mult), [a])
                        wkr.done(wi, th)
                    slot_free[s] = tk
                for p in range(2 * hh, 2 * hh + 2):
                    for ob in range(4):
                        s, lt = load_slot([wblock(w_down[l][p * 2048:(p + 1) * 2048, ob * 512:(ob + 1) * 512], 16, 512)])
                        wv = wview(s, 16, 512)
                        fin_blk = (p == 3 and ob == 3)
                        if fin_blk:
                            outs = None
                            if l == DEPTH - 1 and out_rows is not None:
                                outs = [out[out_rows + t * 128:out_rows + (t + 1) * 128, :] for t in range(nt)]
                            ln_tile, ln_end = ln_epilogue(l, 2, nt, last_layer_out=outs,
                                                          need_T=(l < DEPTH - 1) or out_rows is None)
                        tk = None
                        for t in range(nt):
                            bi, bap, bdep = banks.get()
                            tk = mm_group(bap, [(hT[:, (p - 2 * hh) * 16 + kc, t * 128:(t + 1) * 128], wv[:, kc, :])
                                                for kc in range(16)], [lt, th], bdep)
                            if p == 0:
                                a = S.op('dve', lambda e, bap=bap, t=t, ob=ob: e.scalar_tensor_tensor(
                                    out=xres[:, t, ob * 512:(ob + 1) * 512], in0=xres[:, t, ob * 512:(ob + 1) * 512], scalar=ALPHA,
                                    in1=bap, op0=ALU.mult, op1=ALU.add), [tk, state['xres'][t]])
                            else:
                                a = S.op('dve', lambda e, bap=bap, t=t, ob=ob: e.tensor_tensor(
                                    out=xres[:, t, ob * 512:(ob + 1) * 512], in0=xres[:, t, ob * 512:(ob + 1) * 512],
                                    in1=bap, op=ALU.add), [tk, state['xres'][t]])
                            banks.done(bi, a)
                            state['xres'][t] = acc_stats(t, ob, a) if p == 3 else a
                            if fin_blk:
                                ln_tile(t)
                        slot_free[s] = tk
            ln_end()
            if DEBUG.get('stop') == 'ln3' and DEBUG.get('gl') == (row0, l):
                return 'stop'
            return None

        def load_group(row0, nt):
            for t in range(nt):
                d = S.dma('sp', 'dx%d' % t, xres[:, t, :], xin[row0 + t * 128:row0 + (t + 1) * 128, :], [state['xres'][t]])
                xi, xbuf, xdep = xbr.get()
                d2 = S.op('act', lambda e, t=t, xbuf=xbuf: e.activation(out=xbuf, in_=xres[:, t, :], func=AF.Copy),
                          [d, xdep])
                tl = None
                cs = []
                for hb in range(2):
                    bi, bap, bdep = tbanks.get()
                    for j in range(8):
                        kc = hb * 8 + j
                        tl = S.op('pe', lambda e, bap=bap, j=j, kc=kc, xbuf=xbuf: e.transpose(
                            bap[:, j * 128:(j + 1) * 128], xbuf[:, kc * 128:(kc + 1) * 128], ident),
                            [d2, bdep] + state['pro'] if j == 0 else [], signal=(j == 7))
                    c = S.op('dve', lambda e, bap=bap, hb=hb, t=t: e.tensor_copy(
                        out=xT[:, hb * 8:(hb + 1) * 8, t * 128:(t + 1) * 128],
                        in_=bap.rearrange("p (c t) -> p c t", c=8)), [tl, state['pe_last']])
                    tbanks.done(bi, c)
                    cs.append(c)
                xbr.done(xi, tl)
                state['xT_w'][t] = tuple(cs)
                state['xres'][t] = d2

        plan = DEBUG.get('plan')
        if plan is None:
            plan = [('H', 0)] + [('G', g) for g in range(4)]
        stopped = False
        for kind, g in plan:
            if stopped or DEBUG.get('stop') == 'pro':
                break
            if kind == 'H':
                load_group(0, 2)
                r = group_layer(0, 2, 0, False, False, None)
                if r == 'stop':
                    stopped = True
                    break
                group_layer(1, 2, 0, True, False, None)
            else:
                row0 = 256 + g * 512
                load_group(row0, 4)
                if DEBUG.get('stop') == 'load':
                    break
                for l in range(2):
                    r = group_layer(l, 4, row0, False, g == 0, g * 512)
                    if r == 'stop':
                        stopped = True
                        break
        fin = []
        if DEBUG:
            for t in range(4):
                fin.append(S.dma('sp', 'do%d' % t, dbg_out[:, t * D:(t + 1) * D], xres[:, t, :], [state['xres'][t]]))
        S.op('sp', lambda e: e.nop(), [('do%d' % t, S.cnt['do%d' % t]) for t in range(4) if ('do%d' % t) in S.cnt],
             signal=False)
        S.emit(es)
    return nc


def _host_inputs(inputs):
    f32 = np.float32
    x = np.asarray(inputs["x"], f32)
    mem = np.asarray(inputs["mem"], f32)
    w_in = np.ascontiguousarray(np.asarray(inputs["w_in"], f32))
    OFF_K = 3072
    OFF_VA = 3328
    wk = w_in[:, :, OFF_K:OFF_K + 256].reshape(2, D, 4, 1, 64)
    w_kd = np.ascontiguousarray(np.broadcast_to(wk, (2, D, 4, 2, 64)).reshape(2, D, 512))
    wv = w_in[:, :, OFF_VA:OFF_VA + 256].reshape(2, D, 4, 1, 64)
    w_vd = np.ascontiguousarray(np.broadcast_to(wv, (2, D, 4, 2, 64)).reshape(2, D, 512))
    b_gate = np.asarray(inputs["b_gate"], f32)
    bgT = np.ascontiguousarray(b_gate.reshape(2, 32, 128).transpose(2, 0, 1).reshape(128, 64))
    w_s = np.asarray(inputs["w_s"], f32)
    wsT = np.ascontiguousarray(w_s.transpose(0, 3, 1, 2).reshape(2, 128, 1024))
    b_s = np.ascontiguousarray(np.asarray(inputs["b_s"], f32).reshape(2, 1024))
    sk = np.asarray(inputs["sinks"], f32).reshape(2, 4, 2, 2)
    sinks = np.ascontiguousarray(sk.transpose(0, 1, 3, 2).reshape(1, 32))
    ident = np.eye(128, dtype=f32)
    rperm = np.zeros((128, 128), f32)
    for dp in range(128):
        partner = dp + 32 if (dp % 64) < 32 else dp - 32
        rperm[partner, dp] = 1.0
    kk = np.arange(128)[:, None]
    qq = np.arange(128)[None, :]
    maskC = (kk <= qq).astype(f32)
    maskP = (kk > qq).astype(f32)
    inv = (1.0 / (10000.0 ** (np.arange(0, 64, 2, dtype=f32) / f32(64)))).astype(f32)
    common = dict(
        w_in=w_in, w_kd=w_kd, w_vd=w_vd,
        w_br_a=np.asarray(inputs["w_br_a"], f32), w_br_b=np.asarray(inputs["w_br_b"], f32),
        w_o=np.asarray(inputs["w_o"], f32), w_xq=np.asarray(inputs["w_xq"], f32),
        w_xkv=np.asarray(inputs["w_xkv"], f32), w_xo=np.asarray(inputs["w_xo"], f32),
        w_up=np.asarray(inputs["w_up"], f32), w_down=np.asarray(inputs["w_down"], f32),
        lnv_g=np.asarray(inputs["ln_v_g"], f32), lnv_b=np.asarray(inputs["ln_v_b"], f32), b_s=b_s,
        ln1_g=np.asarray(inputs["ln1_g"], f32), ln1_b=np.asarray(inputs["ln1_b"], f32),
        ln2_g=np.asarray(inputs["ln2_g"], f32), ln2_b=np.asarray(inputs["ln2_b"], f32),
        ln3_g=np.asarray(inputs["ln3_g"], f32), ln3_b=np.asarray(inputs["ln3_b"], f32),
        sinks=sinks, bgT=bgT, wsT=wsT,
    )
    in_maps = []
    for c in range(8):
        b, half = c // 2, c % 2
        start = half * 2048
        xin = np.zeros((2304, D), f32)
        if half == 1:
            xin[:] = x[b, start - 256:start + 2048]
        else:
            xin[256:] = x[b, 0:2048]
        pos = np.maximum(np.arange(start - 256, start + 2048), 0).astype(f32)
        ang = pos[:, None] * inv[None, :]
        cos = np.cos(ang).astype(f32).T
        sin = np.sin(ang).astype(f32).T
        cosT = np.concatenate([cos, cos, cos, cos], axis=0)
        sinS = np.concatenate([-sin, sin, -sin, sin], axis=0)
        ropetab = np.ascontiguousarray(np.stack([cosT, sinS], axis=0))
        maskP0 = maskP if half == 1 else np.zeros_like(maskP)
        consts = np.ascontiguousarray(np.concatenate([ident, rperm, maskC, maskP, maskP0], axis=1))
        m = dict(common)
        m.update(xin=xin, mem=np.ascontiguousarray(mem[b]), ropetab=ropetab, consts=consts)
        in_maps.append(m)
    return in_maps


_NC_CACHE = {}


def kernel(**inputs):
    in_maps = _host_inputs(inputs)
    if 'nc' not in _NC_CACHE:
        _NC_CACHE['nc'] = build()
    nc = _NC_CACHE['nc']
    res = run_bass_kernel_spmd(nc, in_maps, core_ids=list(range(8)))
    outp = np.zeros((4, 4096, D), np.float32)
    for c in range(8):
        b, half = c // 2, c % 2
        outp[b, half * 2048:(half + 1) * 2048] = res.results[c]["out"]
    if DEBUG:
        kernel.dbg = [res.results[c]["dbg"] for c in range(8)]
    return outp
```

```python
import numpy as np
from contextlib import ExitStack
import concourse.bass as bass
import concourse.mybir as mybir
from concourse.bass_utils import run_bass_kernel_spmd

F32 = mybir.dt.float32
BF16 = mybir.dt.bfloat16
AF = mybir.ActivationFunctionType
ALU = mybir.AluOpType

D = 2048
DEPTH = 2
ALPHA = float((2 * DEPTH) ** 0.25)
EPS = 1e-5
NSLOT = 4
NBANK = 6
ENGS = ['pe', 'act', 'dve', 'pool', 'sp']
OFF_V, OFF_Q, OFF_GA, OFF_GB = 1024, 2048, 3584, 5632

DEBUG = {}


class Sched:
    def __init__(self, nc):
        self.nc = nc
        self.prog = {e: [] for e in ENGS}
        self.cnt = {}
        self.sems = {}

    def op(self, eng, fn, deps=(), signal=True):
        tok = None
        if signal:
            self.cnt[eng] = self.cnt.get(eng, 0) + 1
            tok = (eng, self.cnt[eng])
        self.prog[eng].append((fn, [d for d in deps if d is not None], tok, 1))
        return tok

    def dma(self, eng, semname, out, in_, deps=()):
        self.cnt[semname] = self.cnt.get(semname, 0) + 16
        tok = (semname, self.cnt[semname])
        fn = lambda e, out=out, in_=in_: e.dma_start(out=out, in_=in_)
        self.prog[eng].append((fn, [d for d in deps if d is not None], tok, 16))
        return tok

    def emit(self, es):
        nc = self.nc
        names = set(self.cnt.keys()) | set(ENGS)
        for n in sorted(names):
            self.sems[n] = es.enter_context(nc.semaphore("sem_" + n))
        block = es.enter_context(nc.Block())
        engmap = {'pe': block.tensor, 'act': block.scalar, 'dve': block.vector,
                  'pool': block.gpsimd, 'sp': block.sync}
        for ename in ENGS:
            prog = self.prog[ename]

            def body(e, prog=prog):
                waited = {}
                for fn, deps, tok, inc in prog:
                    for (sn, v) in deps:
                        if waited.get(sn, 0) < v:
                            e.wait_ge(self.sems[sn], v)
                            waited[sn] = v
                    ins = fn(e)
                    if tok is not None:
                        ins.then_inc(self.sems[tok[0]], inc)
            engmap[ename](body)


class Ring:
    def __init__(self, aps):
        self.aps = aps
        self.free = [None] * len(aps)
        self.nxt = 0

    def get(self):
        i = self.nxt % len(self.aps)
        self.nxt += 1
        return i, self.aps[i], self.free[i]

    def done(self, i, tok):
        self.free[i] = tok


def build():
    nc = bass.Bass("TRN2", target_bir_lowering=False)

    def din(name, shape):
        return nc.dram_tensor(name, list(shape), F32, kind="ExternalInput").ap()

    xin = din("xin", [2304, D])
    mem = din("mem", [256, D])
    w_in = din("w_in", [2, D, 7680])
    w_kd = din("w_kd", [2, D, 512])
    w_vd = din("w_vd", [2, D, 512])
    w_br_a = din("w_br_a", [2, 1024, D])
    w_br_b = din("w_br_b", [2, 1024, D])
    w_o = din("w_o", [2, D, D])
    w_xq = din("w_xq", [2, D, 512])
    w_xkv = din("w_xkv", [2, D, 1024])
    w_xo = din("w_xo", [2, 512, D])
    w_up = din("w_up", [2, D, 8192])
    w_down = din("w_down", [2, 8192, D])
    lnv_g = din("lnv_g", [2, 1024])
    lnv_b = din("lnv_b", [2, 1024])
    b_s = din("b_s", [2, 1024])
    lng = [din("ln%d_g" % i, [2, D]) for i in (1, 2, 3)]
    lnb = [din("ln%d_b" % i, [2, D]) for i in (1, 2, 3)]
    sinks = din("sinks", [1, 32])
    bgT = din("bgT", [128, 64])
    wsT = din("wsT", [2, 128, 1024])
    consts = din("consts", [128, 5 * 128])
    ropetab = din("ropetab", [2, 128, 2304])
    out = nc.dram_tensor("out", [2048, D], F32, kind="ExternalOutput").ap()
    dbg_out = None
    if DEBUG:
        dbg_out = nc.dram_tensor("dbg", [128, 4 * D], F32, kind="ExternalOutput").ap()

    es = ExitStack()
    with es:
        def sb(name, shape, dt):
            return es.enter_context(nc.sbuf_tensor(name, list(shape), dt))

        xres = sb("xres", [128, 4, D], F32)
        xT = sb("xT", [128, 16, 512], BF16)
        R = sb("R", [128, 24576], BF16)
        W = sb("W", [128, NSLOT, 4096], F32)
        xb = sb("xb", [128, D], BF16)
        cb = sb("cb", [128, 5, 128], BF16)
        cf = sb("cf", [128, 5 * 128], F32)
        ones = sb("ones", [128, 128], BF16)
        rope = sb("rope", [128, 2, 512], F32)
        bg = sb("bg", [128, 64], F32)
        esk = sb("esk", [128, 32], F32)
        WsT = sb("WsT", [128, 2, 1024], BF16)
        kxT = sb("kxT", [128, 2, 4, 256], BF16)
        vx = sb("vx", [128, 2, 2, 512], BF16)
        kprev = sb("kprev", [128, 2, 4, 128], BF16)
        vprev = sb("vprev", [128, 2, 512], BF16)
        wk = sb("wk", [128, 3, 512], F32)
        vg = sb("vg", [128, 1, 1024], F32)
        vn = sb("vn", [128, 2, 1024], BF16)
        pt = sb("pt", [128, 4, 512], BF16)
        stats = sb("stats", [128, 4, 6], F32)
        stats4 = sb("stats4", [128, 4, 4, 6], F32)
        mv = sb("mv", [128, 8], F32)
        epsc = sb("epsc", [128, 1], F32)
        mb = sb("mb", [128, 3, 128], BF16)
        psb = [es.enter_context(nc.psum_tensor("ps%d" % i, [128, 512], F32)) for i in range(NBANK)]
        ptr = [es.enter_context(nc.psum_tensor("ptr%d" % i, [128, 1024], BF16)) for i in range(2)]

        S = Sched(nc)
        banks = Ring([p[:] for p in psb])
        tbanks = Ring([p[:] for p in ptr])
        wkr = Ring([wk[:, i, :] for i in range(3)])
        ptr_ring = Ring([pt[:, i, :] for i in range(4)])
        vgr = Ring([vg[:, i, :] for i in range(1)])
        vnr = Ring([vn[:, i, :] for i in range(2)])
        xbr = Ring([xb[:], vg[:, 0, :].bitcast(BF16)])

        ident = cb[:, 0, :]
        rperm = cb[:, 1, :]
        maskC = cb[:, 2, :]
        maskP = cb[:, 3, :]
        maskP0 = cb[:, 4, :]

        def rv(off, c, n=512):
            return R[:, off:off + c * n].rearrange("p (c t) -> p c t", c=c)
        uT = rv(0, 8)
        qT = rv(4096, 8)
        mg = rv(0, 16)
        zT = rv(8192, 8)
        oT = rv(12288, 8)
        kTa = R[:, 16384:16384 + 4 * 640].rearrange("p (c t) -> p c t", c=4)
        Vd = R[:, 18944:18944 + 5 * 512].rearrange("p (c t) -> p c t", c=5)
        kTb = R[:, 21504:21504 + 4 * 640].rearrange("p (c t) -> p c t", c=4)
        hT = rv(0, 32)
        xqT = rv(0, 4)
        oxT = rv(2048, 4)
        memT = R[:, 0:16 * 256].rearrange("p (c t) -> p c t", c=16)

        slot_free = [None] * NSLOT
        slot_n = [0]

        def slot_bf(s):
            return W[:, s, :].bitcast(BF16)

        def load_slot(parts, queue='pool'):
            s = slot_n[0] % NSLOT
            slot_n[0] += 1
            tok = None
            for dst_fn, src in parts:
                tok = S.dma(queue, ('w%d' if queue == 'pool' else 'v%d') % s, dst_fn(s), src, [slot_free[s]])
            return s, tok

        def wblock(src, kc, n, off=0):
            return (lambda s: slot_bf(s)[:, off:off + kc * n].rearrange("p (k n) -> p k n", k=kc),
                    src.rearrange("(k p) n -> p k n", p=128))

        def wview(s, kc, n, off=0):
            return slot_bf(s)[:, off:off + kc * n].rearrange("p (k n) -> p k n", k=kc)

        state = {'xres': [None] * 4, 'xT_w': [None] * 4, 'pe_last': None}

        def mm_group(out_ap, pairs, deps, bank_dep):
            n = len(pairs)
            tok = None
            for i, (l, r) in enumerate(pairs):
                tok = S.op('pe', lambda e, l=l, r=r, i=i: e.matmul(out_ap, lhsT=l, rhs=r, start=(i == 0),
                                                                  stop=(i == n - 1)),
                           (list(deps) + [bank_dep]) if i == 0 else [], signal=(i == n - 1))
            state['pe_last'] = tok
            return tok

        def ln_epilogue(l, which, nt, last_layer_out=None, need_T=True):
            s, ltok = load_slot([
                (lambda s: W[:, s, 0:D], lng[which][l].partition_broadcast(128)),
                (lambda s: W[:, s, D:2 * D], lnb[which][l].partition_broadcast(128))], queue='sp')
            gbc = W[:, s, 0:D]
            bbc = W[:, s, D:2 * D]
            lst = {'last': None}

            def stage_a(t):
                xt_ = xres[:, t, :]
                d = state['xres'][t]
                d = S.op('dve', lambda e, t=t: e.bn_aggr(out=mv[:, 0:2], in_=stats4[:, t, :, :].rearrange("p a b -> p (a b)")),
                         [d, lst['last']])
                d = S.op('act', lambda e: e.activation(out=mv[:, 2:3], in_=mv[:, 1:2], func=AF.Sqrt, bias=epsc[:, 0:1]), [d])
                d = S.op('dve', lambda e: e.reciprocal(out=mv[:, 2:3], in_=mv[:, 2:3]), [d])
                d = S.op('dve', lambda e, xt_=xt_: e.scalar_tensor_tensor(out=xt_, in0=xt_, scalar=mv[:, 0:1], in1=gbc,
                                                                           op0=ALU.subtract, op1=ALU.mult), [d, ltok])
                d = S.op('dve', lambda e, xt_=xt_: e.scalar_tensor_tensor(out=xt_, in0=xt_, scalar=mv[:, 2:3], in1=bbc,
                                                                           op0=ALU.mult, op1=ALU.add), [d])
                lst['last'] = d
                if last_layer_out is not None and last_layer_out[t] is not None:
                    d = S.dma('sp', 'do%d' % t, last_layer_out[t], xt_, [d])
                info = None
                if need_T:
                    xi, xbuf, xdep = xbr.get()
                    d2 = S.op('act', lambda e, xt_=xt_, xbuf=xbuf: e.activation(out=xbuf, in_=xt_, func=AF.Copy),
                              [lst['last'], xdep])
                    info = (xi, xbuf, d2)
                    d = d2
                state['xres'][t] = d
                return info

            def stage_b(t, info):
                xi, xbuf, d2 = info
                tl = None
                cs = []
                for hb in range(2):
                    bi, bap, bdep = tbanks.get()
                    for j in range(8):
                        kc = hb * 8 + j
                        tl = S.op('pe', lambda e, bap=bap, j=j, kc=kc, xbuf=xbuf: e.transpose(
                            bap[:, j * 128:(j + 1) * 128], xbuf[:, kc * 128:(kc + 1) * 128], ident),
                            [d2, bdep] if j == 0 else [], signal=(j == 7))
                    c = S.op('act', lambda e, bap=bap, hb=hb, t=t: e.activation(
                        out=xT[:, hb * 8:(hb + 1) * 8, t * 128:(t + 1) * 128],
                        in_=bap.rearrange("p (c t) -> p c t", c=8), func=AF.Copy), [tl, state['pe_last']])
                    tbanks.done(bi, c)
                    cs.append(c)
                xbr.done(xi, tl)
                state['xT_w'][t] = tuple(cs)

            ctx = {'prev': None}

            def ln_tile(t):
                info = stage_a(t)
                if ctx['prev'] is not None and ctx['prev'][1] is not None:
                    stage_b(*ctx['prev'])
                ctx['prev'] = (t, info)

            def ln_end():
                if ctx['prev'][1] is not None:
                    stage_b(*ctx['prev'])
                slot_free[s] = lst['last']

            return ln_tile, ln_end

        def acc_stats(t, ob, a):
            return S.op('dve', lambda e, t=t, ob=ob: e.bn_stats(out=stats4[:, t, ob, :], in_=xres[:, t, ob * 512:(ob + 1) * 512]), [a])

        def xT_deps(nt):
            r = []
            for t in range(nt):
                if state['xT_w'][t] is not None:
                    r += list(state['xT_w'][t])
            return r

        t_c = S.dma('sp', 'dc', cf[:], consts)
        t_cb = S.op('act', lambda e: e.activation(out=cb[:].rearrange("p a b -> p (a b)"), in_=cf[:], func=AF.Copy), [t_c])
        t_ones = S.op('dve', lambda e: e.memset(ones[:], 1.0))
        t_ones = S.op('dve', lambda e: e.memset(epsc[:], EPS), [t_ones])
        t_mb = None
        for i in range(3):
            t_mb = S.op('dve', lambda e, i=i: e.tensor_scalar(out=mb[:, i, :], in0=cf[:, (2 + i) * 128:(3 + i) * 128], scalar1=-1.0,
                                                             scalar2=30000.0, op0=ALU.add, op1=ALU.mult), [t_c, t_mb])
        t_bg = S.dma('sp', 'dbg', bg[:], bgT)
        t_sk = S.dma('sp', 'dsk', esk[:], sinks.partition_broadcast(128))
        t_esk = S.op('act', lambda e: e.activation(out=esk[:], in_=esk[:], func=AF.Exp), [t_sk])
        t_kp = S.op('dve', lambda e: e.memset(kprev[:].rearrange("p a b c -> p (a b c)"), 0.0))
        t_vp = S.op('dve', lambda e: e.memset(vprev[:].rearrange("p a b -> p (a b)"), 0.0))
        state['kprev'] = [t_kp, t_kp]
        state['vprev'] = [t_vp, t_vp]
        t_ws_all = []
        for l in range(2):
            i, wap, wdep = wkr.get()
            i2, wap2, wdep2 = wkr.get()
            t1 = S.dma('sp', 'dws%d' % l, wap, wsT[l][:, 0:512], [wdep])
            t2 = S.dma('sp', 'dws%d' % l, wap2, wsT[l][:, 512:1024], [wdep2])
            a = S.op('dve', lambda e, wap=wap, l=l: e.tensor_tensor(
                out=WsT[:, l, 0:512].rearrange("p (g t) -> p g t", g=4), in0=wap.rearrange("p (g t) -> p g t", g=4),
                in1=cf[:, 256:384].unsqueeze(1).to_broadcast([128, 4, 128]), op=ALU.mult), [t2, t_c])
            b = S.op('dve', lambda e, wap2=wap2, l=l: e.tensor_tensor(
                out=WsT[:, l, 512:1024].rearrange("p (g t) -> p g t", g=4), in0=wap2.rearrange("p (g t) -> p g t", g=4),
                in1=cf[:, 256:384].unsqueeze(1).to_broadcast([128, 4, 128]), op=ALU.mult), [a])
            wkr.done(i, a)
            wkr.done(i2, b)
            t_ws_all.append(b)
        mt = []
        for mc in range(2):
            d = S.dma('sp', 'dx%d' % mc, xres[:, mc, :], mem[mc * 128:(mc + 1) * 128, :])
            xi, xbuf, xdep = xbr.get()
            d2 = S.op('act', lambda e, mc=mc, xbuf=xbuf: e.activation(out=xbuf, in_=xres[:, mc, :], func=AF.Copy),
                      [d, xdep])
            tl = None
            for hb in range(2):
                bi, bap, bdep = tbanks.get()
                for j in range(8):
                    kc = hb * 8 + j
                    tl = S.op('pe', lambda e, bap=bap, j=j, kc=kc, xbuf=xbuf: e.transpose(
                        bap[:, j * 128:(j + 1) * 128], xbuf[:, kc * 128:(kc + 1) * 128], ident),
                        [d2, bdep, t_cb] if j == 0 else [], signal=(j == 7))
                c = S.op('dve', lambda e, bap=bap, hb=hb, mc=mc: e.tensor_copy(
                    out=memT[:, hb * 8:(hb + 1) * 8, mc * 128:(mc + 1) * 128],
                    in_=bap.rearrange("p (c t) -> p c t", c=8)), [tl])
                tbanks.done(bi, c)
                mt.append(c)
            xbr.done(xi, tl)
            state['xres'][mc] = tl
        for l in range(2):
            s, lt = load_slot([wblock(w_xkv[l][:, 0:512], 16, 512)])
            wv = wview(s, 16, 512)
            tk = None
            for h in range(4):
                bi, bap, bdep = banks.get()
                tk = mm_group(bap[:, 0:256], [(wv[:, kc, h * 128:(h + 1) * 128], memT[:, kc, :]) for kc in range(16)],
                              [lt] + mt, bdep)
                c = S.op('act', lambda e, bap=bap, l=l, h=h: e.activation(out=kxT[:, l, h, :], in_=bap[:, 0:256],
                                                                         func=AF.Copy), [tk])
                banks.done(bi, c)
            slot_free[s] = tk
            s, lt = load_slot([wblock(w_xkv[l][:, 512:1024], 16, 512)])
            wv = wview(s, 16, 512)
            for mc in range(2):
                bi, bap, bdep = banks.get()
                tk = mm_group(bap, [(memT[:, kc, mc * 128:(mc + 1) * 128], wv[:, kc, :]) for kc in range(16)],
                              [lt] + mt, bdep)
                c = S.op('act', lambda e, bap=bap, l=l, mc=mc: e.activation(out=vx[:, l, mc, :], in_=bap,
                                                                          func=AF.Copy), [tk])
                banks.done(bi, c)
            slot_free[s] = tk
        state['pro'] = [t_cb, t_ones, t_bg, t_esk, t_mb] + t_ws_all

        def rope_evac(bap, tk, dst, n, rtok):
            i0, qs, qd = ptr_ring.get()
            a = S.op('act', lambda e: e.activation(out=qs[:, 0:n], in_=bap[:, 0:n], func=AF.Copy), [tk, qd])
            i1, t1, t1d = wkr.get()
            b = S.op('dve', lambda e: e.tensor_tensor(out=t1[:, 0:n], in0=bap[:, 0:n], in1=rope[:, 0, 0:n], op=ALU.mult),
                     [tk, t1d, rtok, a])
            b2i, bap2, bdep2 = banks.get()
            r = mm_group(bap2[:, 0:n], [(rperm, qs[:, 0:n])], [a], bdep2)
            ptr_ring.done(i0, r)
            i2, t2, t2d = wkr.get()
            c = S.op('dve', lambda e: e.tensor_tensor(out=t2[:, 0:n], in0=bap2[:, 0:n], in1=rope[:, 1, 0:n], op=ALU.mult),
                     [r, t2d])
            banks.done(b2i, c)
            if isinstance(dst, list):
                d = None
                for (dap, p0, p1) in dst:
                    d = S.op('dve', lambda e, dap=dap, p0=p0, p1=p1: e.tensor_tensor(
                        out=dap, in0=t1[p0:p1, 0:n], in1=t2[p0:p1, 0:n], op=ALU.add), [b, c, d])
            else:
                d = S.op('dve', lambda e: e.tensor_tensor(out=dst, in0=t1[:, 0:n], in1=t2[:, 0:n], op=ALU.add), [b, c])
            wkr.done(i1, d)
            wkr.done(i2, d)
            return b, d

        def group_layer(l, nt, row0, kv_only, first_mask_special, out_rows):
            n = nt * 128
            pro = state['pro']
            rtok = S.dma('sp', 'drope', rope[:, 0, 0:n], ropetab[0][:, row0:row0 + n], [state.get('rope_r')])
            rtok = S.dma('sp', 'drope', rope[:, 1, 0:n], ropetab[1][:, row0:row0 + n], [state.get('rope_r')])
            xd = xT_deps(nt)
            if DEBUG.get('stop') == 'rope':
                return 'stop'
            if not kv_only:
                tu = None
                for ub in range(2):
                    s, lt = load_slot([wblock(w_in[l][:, ub * 512:(ub + 1) * 512], 16, 512)])
                    wv = wview(s, 16, 512)
                    tk = None
                    for hc in range(4):
                        bi, bap, bdep = banks.get()
                        tk = mm_group(bap[:, 0:n], [(wv[:, kc, hc * 128:(hc + 1) * 128], xT[:, kc, 0:n]) for kc in range(16)],
                                      [lt] + xd, bdep)
                        tu = S.op('act', lambda e, bap=bap, c=ub * 4 + hc: e.activation(
                            out=uT[:, c, 0:n], in_=bap[:, 0:n], func=AF.Gelu_apprx_tanh), [tk])
                        banks.done(bi, tu)
                    slot_free[s] = tk
                if DEBUG.get('stop') == 'A1':
                    return 'stop'
                s0, lt0 = load_slot([wblock(w_in[l][:, OFF_V:OFF_V + 512], 16, 512)])
                s1, lt1 = load_slot([wblock(w_in[l][:, OFF_V + 512:OFF_V + 1024], 16, 512)])
                sv, ltv = load_slot([
                    (lambda s: W[:, s, 0:1024], lnv_g[l].partition_broadcast(128)),
                    (lambda s: W[:, s, 1024:2048], lnv_b[l].partition_broadcast(128)),
                    (lambda s: W[:, s, 2048:3072], b_s[l].partition_broadcast(128))], queue='sp')
                wvs = [wview(s0, 16, 512), wview(s1, 16, 512)]
                a2 = {'tkv': None, 'tz': None}

                def v_stage(t):
                    gi, gap, gdep = vgr.get()
                    d = None
                    for vb in range(2):
                        bi, bap, bdep = banks.get()
                        a2['tkv'] = mm_group(bap, [(xT[:, kc, t * 128:(t + 1) * 128], wvs[vb][:, kc, :]) for kc in range(16)],
                                             [lt0, lt1] + xd, bdep)
                        a = S.op('act', lambda e, bap=bap, gap=gap, vb=vb: e.activation(
                            out=gap[:, vb * 512:(vb + 1) * 512], in_=bap, func=AF.Gelu_apprx_tanh), [a2['tkv'], gdep])
                        banks.done(bi, a)
                        d = S.op('dve', lambda e, gap=gap, vb=vb: e.bn_stats(out=stats[:, vb, :], in_=gap[:, vb * 512:(vb + 1) * 512]),
                                 [a, d, state.get('stats_r')])
                    d = S.op('dve', lambda e: e.bn_aggr(out=mv[:, 4:6], in_=stats[:, 0:2, :].rearrange("p a b -> p (a b)")), [d])
                    state['stats_r'] = d
                    d = S.op('act', lambda e: e.activation(out=mv[:, 6:7], in_=mv[:, 5:6], func=AF.Sqrt, bias=epsc[:, 0:1]), [d])
                    d = S.op('dve', lambda e: e.reciprocal(out=mv[:, 6:7], in_=mv[:, 6:7]), [d])
                    d = S.op('dve', lambda e, gap=gap: e.scalar_tensor_tensor(out=gap, in0=gap, scalar=mv[:, 4:5], in1=W[:, sv, 0:1024],
                                                                             op0=ALU.subtract, op1=ALU.mult), [d, ltv])
                    ni, nap, ndep = vnr.get()
                    d = S.op('dve', lambda e, gap=gap, nap=nap: e.scalar_tensor_tensor(out=nap, in0=gap, scalar=mv[:, 6:7],
                                                                                      in1=W[:, sv, 1024:2048], op0=ALU.mult,
                                                                                      op1=ALU.add), [d, ndep])
                    vgr.done(gi, d)
                    return (t, ni, nap, d)

                def a3_stage(t, ni, nap, d):
                    tm = None
                    for half in range(2):
                        bi, bap, bdep = banks.get()
                        for gq in range(4):
                            g = half * 4 + gq
                            tm = mm_group(bap[:, gq * 128:(gq + 1) * 128],
                                          [(nap[:, g * 128:(g + 1) * 128], WsT[:, l, g * 128:(g + 1) * 128])],
                                          [d] + pro, bdep if gq == 0 else None)
                        wi, wap, wdep = wkr.get()
                        a = S.op('dve', lambda e, bap=bap, wap=wap, half=half: e.tensor_tensor(
                            out=wap, in0=bap, in1=W[:, sv, 2048 + half * 512:2048 + (half + 1) * 512], op=ALU.add),
                            [tm, wdep, ltv])
                        banks.done(bi, a)
                        a2['tz'] = S.op('dve', lambda e, wap=wap, half=half, t=t: e.tensor_tensor(
                            out=zT[:, half * 4:(half + 1) * 4, t * 128:(t + 1) * 128],
                            in0=wap.rearrange("p (g t) -> p g t", g=4),
                            in1=uT[:, half * 4:(half + 1) * 4, t * 128:(t + 1) * 128], op=ALU.mult), [a, tu])
                        wkr.done(wi, a2['tz'])
                    vnr.done(ni, tm)

                pv = None
                for t in range(nt):
                    cur = v_stage(t)
                    if pv is not None:
                        a3_stage(*pv)
                    pv = cur
                a3_stage(*pv)
                tkv = a2['tkv']
                tz = a2['tz']
                slot_free[s0] = tkv
                slot_free[s1] = tkv
                slot_free[sv] = tz
                if DEBUG.get('stop') == 'A3':
                    return 'stop'
                tq = None
                for qb in range(2):
                    s, lt = load_slot([wblock(w_in[l][:, OFF_Q + qb * 512:OFF_Q + (qb + 1) * 512], 16, 512)])
                    wv = wview(s, 16, 512)
                    tk = None
                    for hc in range(4):
                        bi, bap, bdep = banks.get()
                        tk = mm_group(bap[:, 0:n], [(wv[:, kc, hc * 128:(hc + 1) * 128], xT[:, kc, 0:n]) for kc in range(16)],
                                      [lt] + xd, bdep)
                        b, tq = rope_evac(bap, tk, qT[:, qb * 4 + hc, 0:n], n, rtok)
                        banks.done(bi, tq)
                    slot_free[s] = tk
            if DEBUG.get('stop') == 'A4':
                return 'stop'
            s, lt = load_slot([wblock(w_kd[l], 16, 512)])
            wv = wview(s, 16, 512)
            tkk = []
            zk = S.op('dve', lambda e: e.memset(kTa[64:128, :, :], 0.0), [state['pe_last']])
            zk = S.op('dve', lambda e: e.memset(kTb[0:64, :, :], 0.0), [state['pe_last'], zk])
            cpk = S.op('act', lambda e: e.activation(out=kTa[0:64, :, 0:128], in_=kprev[0:64, l, :, :], func=AF.Copy),
                       [state['kprev'][l], state['pe_last']])
            cpk = S.op('act', lambda e: e.activation(out=kTb[64:128, :, 0:128], in_=kprev[64:128, l, :, :], func=AF.Copy),
                       [state['kprev'][l], state['pe_last'], cpk])
            cpv = S.op('act', lambda e: e.activation(out=Vd[:, 0, :], in_=vprev[:, l, :], func=AF.Copy),
                       [state['vprev'][l], state['pe_last']])
            tk = None
            for m in range(4):
                bi, bap, bdep = banks.get()
                tk = mm_group(bap[:, 0:n], [(wv[:, kc, m * 128:(m + 1) * 128], xT[:, kc, 0:n]) for kc in range(16)],
                              [lt] + xd, bdep)
                b, tkd = rope_evac(bap, tk, [(kTa[0:64, m, 128:128 + n], 0, 64), (kTb[64:128, m, 128:128 + n], 64, 128)], n, rtok)
                banks.done(bi, tkd)
                tkk.append(tkd)
            slot_free[s] = tk
            state['rope_r'] = tkk[-1]
            s, lt = load_slot([wblock(w_vd[l], 16, 512)])
            wv = wview(s, 16, 512)
            tvv = []
            for t in range(nt):
                bi, bap, bdep = banks.get()
                tk = mm_group(bap, [(xT[:, kc, t * 128:(t + 1) * 128], wv[:, kc, :]) for kc in range(16)], [lt] + xd, bdep)
                c = S.op('act', lambda e, bap=bap, t=t: e.activation(out=Vd[:, 1 + t, :], in_=bap, func=AF.Copy), [tk])
                banks.done(bi, c)
                tvv.append(c)
            slot_free[s] = tk
            sk = S.op('act', lambda e: e.activation(out=kprev[0:64, l, :, :], in_=kTa[0:64, :, n:n + 128], func=AF.Copy),
                      tkk + [cpk])
            sk = S.op('act', lambda e: e.activation(out=kprev[64:128, l, :, :], in_=kTb[64:128, :, n:n + 128], func=AF.Copy),
                      tkk + [cpk, sk])
            sv_ = S.op('act', lambda e: e.activation(out=vprev[:, l, :], in_=Vd[:, nt, :], func=AF.Copy), [tvv[-1], cpv])
            state['kprev'][l] = sk
            state['vprev'][l] = sv_
            if kv_only:
                return
            if DEBUG.get('stop') == 'A5':
                return 'stop'
            att = {'last': None}

            def att_stage1(t, m):
                pts = []
                for kb in range(2):
                    bi, bap, bdep = banks.get()
                    if kb == 1:
                        mi = 0
                    else:
                        mi = 2 if (first_mask_special and t == 0) else 1
                    S.op('pe', lambda e, bap=bap, mi=mi: e.matmul(
                        bap.rearrange("p (j q) -> p j q", j=4), lhsT=ident,
                        rhs=mb[:, mi, :].unsqueeze(1).to_broadcast([128, 4, 128]), start=True, stop=False,
                        skip_group_check=True), tkk + [tq, cpk, zk, bdep] + pro, signal=False)
                    tk = None
                    for half in range(2):
                        kTh = kTa if half == 0 else kTb
                        for j in range(2):
                            blk = half * 2 + j
                            tk = S.op('pe', lambda e, bap=bap, blk=blk, kTh=kTh, j=j, kb=kb: e.matmul(
                                bap[:, blk * 128:(blk + 1) * 128], lhsT=kTh[:, m, (t + kb) * 128:(t + kb + 1) * 128],
                                rhs=qT[:, 2 * m + j, t * 128:(t + 1) * 128], start=False, stop=True,
                                skip_group_check=True), [], signal=(blk == 3))
                    state['pe_last'] = tk
                    pi, pap, pdep = ptr_ring.get()
                    a = S.op('act', lambda e, bap=bap, pap=pap: e.activation(out=pap, in_=bap, func=AF.Exp, scale=0.125),
                             [tk, pdep])
                    banks.done(bi, a)
                    pts.append((pi, pap, a))
                return pts

            def att_stage2(t, m, pts):
                bo, bapo, bdepo = banks.get()
                tko = mm_group(bapo, [(Vd[:, t + kb, m * 128:(m + 1) * 128], pts[kb][1]) for kb in range(2)],
                               [pts[0][2], pts[1][2], cpv] + tvv, bdepo)
                bs_, baps, bdeps = banks.get()
                tks = mm_group(baps, [(ones[:], pts[kb][1]) for kb in range(2)], pro, bdeps)
                for kb in range(2):
                    ptr_ring.done(pts[kb][0], tks)
                wi, wap, wdep = wkr.get()
                dd = S.op('dve', lambda e, baps=baps, wap=wap: e.tensor_tensor(
                    out=wap.rearrange("p (b q) -> p b q", b=4), in0=baps.rearrange("p (b q) -> p b q", b=4),
                    in1=esk[:, l * 16 + 4 * m:l * 16 + 4 * m + 4].unsqueeze(2).to_broadcast([128, 4, 128]), op=ALU.add),
                    [tks, wdep] + pro)
                banks.done(bs_, dd)
                dd = S.op('act', lambda e, wap=wap: e.activation(out=wap, in_=wap, func=AF.Ln), [dd])
                dd = S.op('act', lambda e, wap=wap: e.activation(out=wap, in_=wap, func=AF.Exp, scale=-1.0), [dd])
                tl_ = None
                for half in range(2):
                    p0 = half * 64
                    tl_ = S.op('dve', lambda e, bapo=bapo, wap=wap, half=half, p0=p0, m=m, t=t: e.tensor_tensor(
                        out=oT[p0:p0 + 64, 2 * m:2 * m + 2, t * 128:(t + 1) * 128],
                        in0=bapo[p0:p0 + 64, half * 256:(half + 1) * 256].rearrange("p (j q) -> p j q", j=2),
                        in1=wap[p0:p0 + 64, half * 256:(half + 1) * 256].rearrange("p (j q) -> p j q", j=2),
                        op=ALU.mult), [tko, dd])
                banks.done(bo, tl_)
                wkr.done(wi, tl_)
                att['last'] = tl_

            prev_item = None
            for t in range(nt):
                for m in range(4):
                    pts = att_stage1(t, m)
                    if prev_item is not None:
                        att_stage2(*prev_item)
                    prev_item = (t, m, pts)
            att_stage2(*prev_item)
            to_last = att['last']
            if DEBUG.get('stop') == 'A6':
                return 'stop'
            tmg = None
            for i in range(8):
                c0 = i * 256
                sg, ltg = load_slot([wblock(w_in[l][:, OFF_GA + c0:OFF_GA + c0 + 256], 16, 256, 0),
                                     wblock(w_in[l][:, OFF_GB + c0:OFF_GB + c0 + 256], 16, 256, 4096)])
                sa, lta = load_slot([wblock(w_br_a[l][:, c0:c0 + 256], 8, 256, 0),
                                     wblock(w_br_b[l][:, c0:c0 + 256], 8, 256, 2048)])
                wga = wview(sg, 16, 256, 0)
                wgb = wview(sg, 16, 256, 4096)
                wa = wview(sa, 8, 256, 0)
                wb_ = wview(sa, 8, 256, 2048)
                tk = None
                for cc in range(2):
                    c = i * 2 + cc
                    cs = slice(cc * 128, (cc + 1) * 128)
                    b1, ba, d1 = banks.get()
                    tka = mm_group(ba[:, 0:n], [(wa[:, kc, cs], zT[:, kc, 0:n]) for kc in range(8)], [lta, tz], d1)
                    b2, bb, d2 = banks.get()
                    tkb = mm_group(bb[:, 0:n], [(wb_[:, kc, cs], oT[:, kc, 0:n]) for kc in range(8)], [lta, to_last], d2)
                    b3, bga, d3 = banks.get()
                    tkga = mm_group(bga[:, 0:n], [(wga[:, kc, cs], xT[:, kc, 0:n]) for kc in range(16)], [ltg] + xd, d3)
                    b4, bgb, d4 = banks.get()
                    tk = mm_group(bgb[:, 0:n], [(wgb[:, kc, cs], xT[:, kc, 0:n]) for kc in range(16)], [ltg] + xd, d4)
                    w1, sga, wd1 = wkr.get()
                    a1 = S.op('act', lambda e, bga=bga, sga=sga, c=c: e.activation(
                        out=sga[:, 0:n], in_=bga[:, 0:n], func=AF.Sigmoid, bias=bg[:, l * 32 + c:l * 32 + c + 1]),
                        [tkga, wd1] + pro)
                    banks.done(b3, a1)
                    w2, sgb, wd2 = wkr.get()
                    a2 = S.op('act', lambda e, bgb=bgb, sgb=sgb, c=c: e.activation(
                        out=sgb[:, 0:n], in_=bgb[:, 0:n], func=AF.Sigmoid, bias=bg[:, l * 32 + 16 + c:l * 32 + 16 + c + 1]),
                        [tk, wd2] + pro)
                    banks.done(b4, a2)
                    m1 = S.op('dve', lambda e, ba=ba, sga=sga: e.tensor_tensor(out=sga[:, 0:n], in0=ba[:, 0:n], in1=sga[:, 0:n],
                                                                            op=ALU.mult), [tka, a1])
                    banks.done(b1, m1)
                    m2 = S.op('dve', lambda e, bb=bb, sgb=sgb: e.tensor_tensor(out=sgb[:, 0:n], in0=bb[:, 0:n], in1=sgb[:, 0:n],
                                                                            op=ALU.mult), [tkb, a2])
                    banks.done(b2, m2)
                    tmg = S.op('dve', lambda e, sga=sga, sgb=sgb, c=c: e.tensor_tensor(out=mg[:, c, 0:n], in0=sga[:, 0:n],
                                                                                    in1=sgb[:, 0:n], op=ALU.add), [m1, m2])
                    wkr.done(w1, tmg)
                    wkr.done(w2, tmg)
                slot_free[sg] = tk
                slot_free[sa] = tk
            if DEBUG.get('stop') == 'A7':
                return 'stop'
            for ob in range(4):
                s, lt = load_slot([wblock(w_o[l][:, ob * 512:(ob + 1) * 512], 16, 512)])
                wv = wview(s, 16, 512)
                if ob == 3:
                    ln_tile, ln_end = ln_epilogue(l, 0, nt)
                tk = None
                for t in range(nt):
                    bi, bap, bdep = banks.get()
                    tk = mm_group(bap, [(mg[:, kc, t * 128:(t + 1) * 128], wv[:, kc, :]) for kc in range(16)], [lt, tmg], bdep)
                    a = S.op('dve', lambda e, bap=bap, t=t, ob=ob: e.scalar_tensor_tensor(
                        out=xres[:, t, ob * 512:(ob + 1) * 512], in0=xres[:, t, ob * 512:(ob + 1) * 512], scalar=ALPHA,
                        in1=bap, op0=ALU.mult, op1=ALU.add), [tk, state['xres'][t]])
                    banks.done(bi, a)
                    state['xres'][t] = acc_stats(t, ob, a)
                    if ob == 3:
                        ln_tile(t)
                slot_free[s] = tk
            ln_end()
            if DEBUG.get('stop') == 'ln1' and DEBUG.get('gl') == (row0, l):
                return 'stop'
            xd = xT_deps(nt)
            s, lt = load_slot([wblock(w_xq[l], 16, 512)])
            wv = wview(s, 16, 512)
            tk = None
            tqs = []
            for h in range(4):
                bi, bap, bdep = banks.get()
                tk = mm_group(bap[:, 0:n], [(wv[:, kc, h * 128:(h + 1) * 128], xT[:, kc, 0:n]) for kc in range(16)],
                              [lt] + xd, bdep)
                c = S.op('act', lambda e, bap=bap, h=h: e.activation(out=xqT[:, h, 0:n], in_=bap[:, 0:n], func=AF.Copy), [tk])
                banks.done(bi, c)
                tqs.append(c)
            slot_free[s] = tk
            xa = {'tox': None}

            def xa_stage1(h):
                pts = []
                for mc in range(2):
                    bi, bap, bdep = banks.get()
                    tk = mm_group(bap[:, 0:n], [(kxT[:, l, h, mc * 128:(mc + 1) * 128], xqT[:, h, 0:n])], [tqs[h]], bdep)
                    pi, pap, pdep = ptr_ring.get()
                    a = S.op('act', lambda e, bap=bap, pap=pap: e.activation(out=pap[:, 0:n], in_=bap[:, 0:n], func=AF.Exp,
                                                                             scale=float(128 ** -0.5)), [tk, pdep])
                    banks.done(bi, a)
                    pts.append((pi, pap, a))
                return pts

            def xa_stage2(h, pts):
                bo, bapo, bdepo = banks.get()
                tko = mm_group(bapo[:, 0:n], [(vx[:, l, mc, h * 128:(h + 1) * 128], pts[mc][1][:, 0:n]) for mc in range(2)],
                               [pts[0][2], pts[1][2]], bdepo)
                bs_, baps, bdeps = banks.get()
                tks = mm_group(baps[:, 0:n], [(ones[:], pts[mc][1][:, 0:n]) for mc in range(2)], pro, bdeps)
                for mc in range(2):
                    ptr_ring.done(pts[mc][0], tks)
                wi, wap, wdep = wkr.get()
                dd = S.op('act', lambda e, baps=baps, wap=wap: e.activation(out=wap[:, 0:n], in_=baps[:, 0:n], func=AF.Ln), [tks, wdep])
                banks.done(bs_, dd)
                dd = S.op('act', lambda e, wap=wap: e.activation(out=wap[:, 0:n], in_=wap[:, 0:n], func=AF.Exp, scale=-1.0), [dd])
                tx_ = S.op('dve', lambda e, bapo=bapo, wap=wap, h=h: e.tensor_tensor(
                    out=oxT[:, h, 0:n], in0=bapo[:, 0:n], in1=wap[:, 0:n], op=ALU.mult), [tko, dd])
                banks.done(bo, tx_)
                wkr.done(wi, tx_)
                xa['tox'] = tx_

            pvh = None
            for h in range(4):
                pts = xa_stage1(h)
                if pvh is not None:
                    xa_stage2(*pvh)
                pvh = (h, pts)
            xa_stage2(*pvh)
            tox = xa['tox']
            s, lt = load_slot([wblock(w_xo[l], 4, 2048)])
            wv = wview(s, 4, 2048)
            ln_tile, ln_end = ln_epilogue(l, 1, nt)
            tk = None
            for t in range(nt):
                for ob in range(4):
                    bi, bap, bdep = banks.get()
                    tk = mm_group(bap, [(oxT[:, kc, t * 128:(t + 1) * 128], wv[:, kc, ob * 512:(ob + 1) * 512]) for kc in range(4)],
                                  [lt, tox], bdep)
                    a = S.op('dve', lambda e, bap=bap, t=t, ob=ob: e.scalar_tensor_tensor(
                        out=xres[:, t, ob * 512:(ob + 1) * 512], in0=xres[:, t, ob * 512:(ob + 1) * 512], scalar=ALPHA,
                        in1=bap, op0=ALU.mult, op1=ALU.add), [tk, state['xres'][t]])
                    banks.done(bi, a)
                    state['xres'][t] = acc_stats(t, ob, a)
                ln_tile(t)
            slot_free[s] = tk
            ln_end()
            if DEBUG.get('stop') == 'ln2' and DEBUG.get('gl') == (row0, l):
                return 'stop'
            xd = xT_deps(nt)
            th = None
            for hh in range(2):
                for hb in range(hh * 8, hh * 8 + 8):
                    s, lt = load_slot([wblock(w_up[l][:, hb * 512:(hb + 1) * 512], 16, 512)])
                    wv = wview(s, 16, 512)
                    tk = None
                    for hc in range(4):
                        bi, bap, bdep = banks.get()
                        tk = mm_group(bap[:, 0:n], [(wv[:, kc, hc * 128:(hc + 1) * 128], xT[:, kc, 0:n]) for kc in range(16)],
                                      [lt] + xd, bdep)
                        wi, wap, wdep = wkr.get()
                        a = S.op('act', lambda e, bap=bap, wap=wap: e.activation(out=wap[:, 0:n], in_=bap[:, 0:n], func=AF.Relu),
                                 [tk, wdep])
                        banks.done(bi, a)
                        th = S.op('dve', lambda e, wap=wap, c=(hb - hh * 8) * 4 + hc: e.tensor_tensor(
                            out=hT[:, c, 0:n], in0=wap[:, 0:n], in1=wap[:, 0:n], op=ALU.mult), [a])
                        wkr.done(wi, th)
                    slot_free[s] = tk
                for p in range(2 * hh, 2 * hh + 2):
                    for ob in range(4):
                        s, lt = load_slot([wblock(w_down[l][p * 2048:(p + 1) * 2048, ob * 512:(ob + 1) * 512], 16, 512)])
                        wv = wview(s, 16, 512)
                        fin_blk = (p == 3 and ob == 3)
                        if fin_blk:
                            outs = None
                            if l == DEPTH - 1 and out_rows is not None:
                                outs = [out[out_rows + t * 128:out_rows + (t + 1) * 128, :] for t in range(nt)]
                            ln_tile, ln_end = ln_epilogue(l, 2, nt, last_layer_out=outs,
                                                          need_T=(l < DEPTH - 1) or out_rows is None)
                        tk = None
                        for t in range(nt):
                            bi, bap, bdep = banks.get()
                            tk = mm_group(bap, [(hT[:, (p - 2 * hh) * 16 + kc, t * 128:(t + 1) * 128], wv[:, kc, :])
                                                for kc in range(16)], [lt, th], bdep)
                            if p == 0:
                                a = S.op('dve', lambda e, bap=bap, t=t, ob=ob: e.scalar_tensor_tensor(
                                    out=xres[:, t, ob * 512:(ob + 1) * 512], in0=xres[:, t, ob * 512:(ob + 1) * 512], scalar=ALPHA,
                                    in1=bap, op0=ALU.mult, op1=ALU.add), [tk, state['xres'][t]])
                            else:
                                a = S.op('dve', lambda e, bap=bap, t=t, ob=ob: e.tensor_tensor(
                                    out=xres[:, t, ob * 512:(ob + 1) * 512], in0=xres[:, t, ob * 512:(ob + 1) * 512],
                                    in1=bap, op=ALU.add), [tk, state['xres'][t]])
                            banks.done(bi, a)
                            state['xres'][t] = acc_stats(t, ob, a) if p == 3 else a
                            if fin_blk:
                                ln_tile(t)
                        slot_free[s] = tk
            ln_end()
            if DEBUG.get('stop') == 'ln3' and DEBUG.get('gl') == (row0, l):
                return 'stop'
            return None

        def load_group(row0, nt):
            for t in range(nt):
                d = S.dma('pool', 'px%d' % t, xres[:, t, :], xin[row0 + t * 128:row0 + (t + 1) * 128, :], [state['xres'][t]])
                xi, xbuf, xdep = xbr.get()
                d2 = S.op('act', lambda e, t=t, xbuf=xbuf: e.activation(out=xbuf, in_=xres[:, t, :], func=AF.Copy),
                          [d, xdep])
                tl = None
                cs = []
                for hb in range(2):
                    bi, bap, bdep = tbanks.get()
                    for j in range(8):
                        kc = hb * 8 + j
                        tl = S.op('pe', lambda e, bap=bap, j=j, kc=kc, xbuf=xbuf: e.transpose(
                            bap[:, j * 128:(j + 1) * 128], xbuf[:, kc * 128:(kc + 1) * 128], ident),
                            [d2, bdep] + state['pro'] if j == 0 else [], signal=(j == 7))
                    c = S.op('dve', lambda e, bap=bap, hb=hb, t=t: e.tensor_copy(
                        out=xT[:, hb * 8:(hb + 1) * 8, t * 128:(t + 1) * 128],
                        in_=bap.rearrange("p (c t) -> p c t", c=8)), [tl, state['pe_last']])
                    tbanks.done(bi, c)
                    cs.append(c)
                xbr.done(xi, tl)
                state['xT_w'][t] = tuple(cs)
                state['xres'][t] = d2

        plan = DEBUG.get('plan')
        if plan is None:
            plan = [('H', 0)] + [('G', g) for g in range(4)]
        stopped = False
        for kind, g in plan:
            if stopped or DEBUG.get('stop') == 'pro':
                break
            if kind == 'H':
                load_group(0, 2)
                r = group_layer(0, 2, 0, False, False, None)
                if r == 'stop':
                    stopped = True
                    break
                group_layer(1, 2, 0, True, False, None)
            else:
                row0 = 256 + g * 512
                load_group(row0, 4)
                if DEBUG.get('stop') == 'load':
                    break
                for l in range(2):
                    r = group_layer(l, 4, row0, False, g == 0, g * 512)
                    if r == 'stop':
                        stopped = True
                        break
        fin = []
        if DEBUG:
            for t in range(4):
                fin.append(S.dma('sp', 'do%d' % t, dbg_out[:, t * D:(t + 1) * D], xres[:, t, :], [state['xres'][t]]))
        S.op('sp', lambda e: e.nop(), [('do%d' % t, S.cnt['do%d' % t]) for t in range(4) if ('do%d' % t) in S.cnt],
             signal=False)
        S.emit(es)
    return nc


def _host_inputs(inputs):
    f32 = np.float32
    x = np.asarray(inputs["x"], f32)
    mem = np.asarray(inputs["mem"], f32)
    w_in = np.ascontiguousarray(np.asarray(inputs["w_in"], f32))
    OFF_K = 3072
    OFF_VA = 3328
    wk = w_in[:, :, OFF_K:OFF_K + 256].reshape(2, D, 4, 1, 64)
    w_kd = np.ascontiguousarray(np.broadcast_to(wk, (2, D, 4, 2, 64)).reshape(2, D, 512))
    wv = w_in[:, :, OFF_VA:OFF_VA + 256].reshape(2, D, 4, 1, 64)
    w_vd = np.ascontiguousarray(np.broadcast_to(wv, (2, D, 4, 2, 64)).reshape(2, D, 512))
    b_gate = np.asarray(inputs["b_gate"], f32)
    bgT = np.ascontiguousarray(b_gate.reshape(2, 32, 128).transpose(2, 0, 1).reshape(128, 64))
    w_s = np.asarray(inputs["w_s"], f32)
    wsT = np.ascontiguousarray(w_s.transpose(0, 3, 1, 2).reshape(2, 128, 1024))
    b_s = np.ascontiguousarray(np.asarray(inputs["b_s"], f32).reshape(2, 1024))
    sk = np.asarray(inputs["sinks"], f32).reshape(2, 4, 2, 2)
    sinks = np.ascontiguousarray(sk.transpose(0, 1, 3, 2).reshape(1, 32))
    ident = np.eye(128, dtype=f32)
    rperm = np.zeros((128, 128), f32)
    for dp in range(128):
        partner = dp + 32 if (dp % 64) < 32 else dp - 32
        rperm[partner, dp] = 1.0
    kk = np.arange(128)[:, None]
    qq = np.arange(128)[None, :]
    maskC = (kk <= qq).astype(f32)
    maskP = (kk > qq).astype(f32)
    inv = (1.0 / (10000.0 ** (np.arange(0, 64, 2, dtype=f32) / f32(64)))).astype(f32)
    common = dict(
        w_in=w_in, w_kd=w_kd, w_vd=w_vd,
        w_br_a=np.asarray(inputs["w_br_a"], f32), w_br_b=np.asarray(inputs["w_br_b"], f32),
        w_o=np.asarray(inputs["w_o"], f32), w_xq=np.asarray(inputs["w_xq"], f32),
        w_xkv=np.asarray(inputs["w_xkv"], f32), w_xo=np.asarray(inputs["w_xo"], f32),
        w_up=np.asarray(inputs["w_up"], f32), w_down=np.asarray(inputs["w_down"], f32),
        lnv_g=np.asarray(inputs["ln_v_g"], f32), lnv_b=np.asarray(inputs["ln_v_b"], f32), b_s=b_s,
        ln1_g=np.asarray(inputs["ln1_g"], f32), ln1_b=np.asarray(inputs["ln1_b"], f32),
        ln2_g=np.asarray(inputs["ln2_g"], f32), ln2_b=np.asarray(inputs["ln2_b"], f32),
        ln3_g=np.asarray(inputs["ln3_g"], f32), ln3_b=np.asarray(inputs["ln3_b"], f32),
        sinks=sinks, bgT=bgT, wsT=wsT,
    )
    in_maps = []
    for c in range(8):
        b, half = c // 2, c % 2
        start = half * 2048
        xin = np.zeros((2304, D), f32)
        if half == 1:
            xin[:] = x[b, start - 256:start + 2048]
        else:
            xin[256:] = x[b, 0:2048]
        pos = np.maximum(np.arange(start - 256, start + 2048), 0).astype(f32)
        ang = pos[:, None] * inv[None, :]
        cos = np.cos(ang).astype(f32).T
        sin = np.sin(ang).astype(f32).T
        cosT = np.concatenate([cos, cos, cos, cos], axis=0)
        sinS = np.concatenate([-sin, sin, -sin, sin], axis=0)
        ropetab = np.ascontiguousarray(np.stack([cosT, sinS], axis=0))
        maskP0 = maskP if half == 1 else np.zeros_like(maskP)
        consts = np.ascontiguousarray(np.concatenate([ident, rperm, maskC, maskP, maskP0], axis=1))
        m = dict(common)
        m.update(xin=xin, mem=np.ascontiguousarray(mem[b]), ropetab=ropetab, consts=consts)
        in_maps.append(m)
    return in_maps


_NC_CACHE = {}


def kernel(**inputs):
    in_maps = _host_inputs(inputs)
    if 'nc' not in _NC_CACHE:
        _NC_CACHE['nc'] = build()
    nc = _NC_CACHE['nc']
    res = run_bass_kernel_spmd(nc, in_maps, core_ids=list(range(8)))
    outp = np.zeros((4, 4096, D), np.float32)
    for c in range(8):
        b, half = c // 2, c % 2
        outp[b, half * 2048:(half + 1) * 2048] = res.results[c]["out"]
    if DEBUG:
        kernel.dbg = [res.results[c]["dbg"] for c in range(8)]
    return outp
```
